# Optimizing a Trainium2 kernel written in Bass

```python
import math
import jax, jax.numpy as jnp
from jax import lax
import numpy as np

D_MODEL = 2048
BATCH = 4
SEQ = 2048
DEPTH = 1

HEAD_DIM = 128
FOX_HEADS = 8
FOX_WIDTH = FOX_HEADS * HEAD_DIM
NSA_HEADS = 8
NSA_KV_GROUPS = 2
NSA_HPG = NSA_HEADS // NSA_KV_GROUPS
NSA_WIDTH = NSA_HEADS * HEAD_DIM
NSA_KV_WIDTH = NSA_KV_GROUPS * HEAD_DIM
N_NSA_BRANCHES = 3
CMP_LEN = 32
CMP_STRIDE = 16
CMP_HIDDEN = 256
SEL_LEN = 64
SEL_TOPK = 8
WINDOW = 512
REL_BUCKETS = 32
REL_MAX_DIST = 128
Q_BLOCK = 128
DEEPNORM_ALPHA = (2 * DEPTH) ** 0.25
DEEPNORM_BETA = (8 * DEPTH) ** -0.25
LN_EPS = 1e-5
NEG = -1e30

COL_LAYOUT = (
    ("fox_q", FOX_WIDTH, 1.0),
    ("fox_k", FOX_WIDTH, 1.0),
    ("fox_v", FOX_WIDTH, DEEPNORM_BETA),
    ("fox_f", FOX_HEADS, 1.0),
    ("fox_z", FOX_WIDTH, 1.0),
    ("nsa_q", NSA_WIDTH, 1.0),
    ("nsa_k_cmp", NSA_KV_WIDTH, 1.0),
    ("nsa_v_cmp", NSA_KV_WIDTH, DEEPNORM_BETA),
    ("nsa_k_sel", NSA_KV_WIDTH, 1.0),
    ("nsa_v_sel", NSA_KV_WIDTH, DEEPNORM_BETA),
    ("nsa_k_win", NSA_KV_WIDTH, 1.0),
    ("nsa_v_win", NSA_KV_WIDTH, DEEPNORM_BETA),
    ("nsa_gate", NSA_HEADS * N_NSA_BRANCHES, 1.0),
    ("nsa_z", NSA_WIDTH, 1.0),
    ("merge_a", D_MODEL, 1.0),
    ("merge_b", D_MODEL, 1.0),
)
IN_COLS = sum(c[1] for c in COL_LAYOUT)

kernel_name = "fox_nsa_gated_hybrid_deepnorm"


def layer_norm(z, g, b):
    zf = z.astype(jnp.float32)
    mu = jnp.mean(zf, axis=-1, keepdims=True)
    var = jnp.mean(jnp.square(zf - mu), axis=-1, keepdims=True)
    return ((zf - mu) * lax.rsqrt(var + LN_EPS) * g + b).astype(z.dtype)


def rel_bucket(dist):
    n = jnp.maximum(dist, 0)
    exact = REL_BUCKETS // 2
    large = exact + (jnp.log(jnp.maximum(n, 1).astype(jnp.float32) / exact)
                     / math.log(REL_MAX_DIST / exact) * (REL_BUCKETS - exact)).astype(jnp.int32)
    return jnp.where(n < exact, n, jnp.minimum(large, REL_BUCKETS - 1))


def fox_attention(q, k, v, log_f):
    B, S, H, dh = q.shape
    nq = S // Q_BLOCK
    c = jnp.cumsum(log_f, axis=1).transpose(0, 2, 1)
    qb = q.reshape(B, nq, Q_BLOCK, H, dh).transpose(1, 0, 2, 3, 4)
    cb = c.reshape(B, H, nq, Q_BLOCK).transpose(2, 0, 1, 3)
    s_pos = jnp.arange(S)
    scale = dh ** -0.5

    def block(args):
        qc, cc, q0 = args
        t_pos = q0 + jnp.arange(Q_BLOCK)
        s = jnp.einsum('bqhd,bshd->bhqs', qc, k).astype(jnp.float32) * scale
        s = s + cc[..., None] - c[:, :, None, :]
        s = jnp.where(s_pos[None, :] <= t_pos[:, None], s, NEG)
        p = jax.nn.softmax(s, axis=-1).astype(v.dtype)
        return jnp.einsum('bhqs,bshd->bqhd', p, v)

    o = lax.map(block, (qb, cb, jnp.arange(nq) * Q_BLOCK))
    return o.transpose(1, 0, 2, 3, 4).reshape(B, S, H, dh)


def nsa_attention(q, kc_raw, vc_raw, ks, vs, kw, vw, gates,
                  cmp_pos_k, cmp_pos_v, cmp_wk1, cmp_wk2, cmp_wv1, cmp_wv2, rel_bias):
    B, S, H, dh = q.shape
    G, HPG = NSA_KV_GROUPS, NSA_HPG
    scale = dh ** -0.5
    qg = q.reshape(B, S, G, HPG, dh)
    t_pos = jnp.arange(S)

    n_cmp = (S - CMP_LEN) // CMP_STRIDE + 1
    cmp_start = jnp.arange(n_cmp) * CMP_STRIDE
    blk_idx = cmp_start[:, None] + jnp.arange(CMP_LEN)[None, :]

    def compress(raw, pos, w1, w2):
        blocks = raw[:, blk_idx] + pos[None, None, :, None, :]
        flat = blocks.transpose(0, 1, 3, 2, 4).reshape(B, n_cmp, G, CMP_LEN * dh)
        return jax.nn.gelu(flat @ w1) @ w2

    k_cmp = compress(kc_raw, cmp_pos_k, cmp_wk1, cmp_wk2)
    v_cmp = compress(vc_raw, cmp_pos_v, cmp_wv1, cmp_wv2)
    blk_end = cmp_start + CMP_LEN - 1
    cmask = blk_end[None, :] <= t_pos[:, None]
    cbias = rel_bias[rel_bucket(t_pos[:, None] - blk_end[None, :])]
    cbias = cbias.transpose(2, 0, 1).reshape(G, HPG, S, n_cmp)
    sc = jnp.einsum('btghd,bcgd->bghtc', qg, k_cmp).astype(jnp.float32) * scale + cbias
    sc = jnp.where(cmask, sc, NEG)
    p_cmp = jax.nn.softmax(sc, axis=-1) * cmask
    o_cmp = jnp.einsum('bghtc,bcgd->btghd', p_cmp.astype(v_cmp.dtype), v_cmp).reshape(B, S, H, dh)

    n_sel = S // SEL_LEN
    sel_start = jnp.arange(n_sel) * SEL_LEN
    overlap = ((cmp_start[:, None] < sel_start[None, :] + SEL_LEN)
               & (cmp_start[:, None] + CMP_LEN > sel_start[None, :])).astype(jnp.float32)
    imp = jnp.einsum('bgtc,cj->bgtj', jnp.sum(p_cmp, axis=2), overlap)
    cur = t_pos // SEL_LEN
    j = jnp.arange(n_sel)
    forced = (j[None, :] == 0) | (j[None, :] == cur[:, None]) | (j[None, :] == cur[:, None] - 1)
    valid = sel_start[None, :] <= t_pos[:, None]
    imp = jnp.where(valid, jnp.where(forced, -NEG, imp), NEG)
    k_top = min(SEL_TOPK, n_sel)
    _, sel_idx = lax.top_k(imp, k_top)

    nq = S // Q_BLOCK
    qb = qg.reshape(B, nq, Q_BLOCK, G, HPG, dh).transpose(1, 0, 2, 3, 4, 5)
    ib = sel_idx.reshape(B, G, nq, Q_BLOCK, k_top).transpose(2, 0, 1, 3, 4)
    ks_blk = ks.reshape(B, n_sel, SEL_LEN, G, dh).transpose(0, 3, 1, 2, 4)
    vs_blk = vs.reshape(B, n_sel, SEL_LEN, G, dh).transpose(0, 3, 1, 2, 4)
    kw_pad = jnp.pad(kw, ((0, 0), (WINDOW, 0), (0, 0), (0, 0)))
    vw_pad = jnp.pad(vw, ((0, 0), (WINDOW, 0), (0, 0), (0, 0)))
    win_len = WINDOW + Q_BLOCK
    qi = jnp.arange(Q_BLOCK)
    kj = jnp.arange(win_len)
    wdist = WINDOW + qi[:, None] - kj[None, :]
    wband = (wdist >= 0) & (wdist < WINDOW)
    wbias = rel_bias[rel_bucket(wdist)].transpose(2, 0, 1).reshape(G, HPG, Q_BLOCK, win_len)
    tbl = rel_bias.T.reshape(G, HPG, REL_BUCKETS)
    gather_blocks = jax.vmap(jax.vmap(lambda blk, idx: blk[idx]))
    group_bias = jax.vmap(jax.vmap(lambda tg, bk: tg[:, bk]), in_axes=(None, 0))

    def block(args):
        qc, ic, q0 = args
        t = q0 + qi
        kg = gather_blocks(ks_blk, ic)
        vg = gather_blocks(vs_blk, ic)
        spos = ic[..., None] * SEL_LEN + jnp.arange(SEL_LEN)
        sdist = t[None, None, :, None, None] - spos
        ss = (jnp.einsum('bqghd,bgqkld->bghqkl', qc, kg).astype(jnp.float32) * scale
              + group_bias(tbl, rel_bucket(sdist)))
        ss = jnp.where((sdist >= 0)[:, :, None], ss, NEG)
        ps = jax.nn.softmax(ss.reshape(ss.shape[:4] + (-1,)), axis=-1).reshape(ss.shape).astype(vg.dtype)
        o_s = jnp.einsum('bghqkl,bgqkld->bqghd', ps, vg)
        kwc = lax.dynamic_slice_in_dim(kw_pad, q0, win_len, axis=1)
        vwc = lax.dynamic_slice_in_dim(vw_pad, q0, win_len, axis=1)
        sw = jnp.einsum('bqghd,bsgd->bghqs', qc, kwc).astype(jnp.float32) * scale + wbias
        wmask = wband & (q0 - WINDOW + kj >= 0)[None, :]
        sw = jnp.where(wmask, sw, NEG)
        pw = jax.nn.softmax(sw, axis=-1).astype(vwc.dtype)
        o_w = jnp.einsum('bghqs,bsgd->bqghd', pw, vwc)
        return o_s, o_w

    o_sel, o_win = lax.map(block, (qb, ib, jnp.arange(nq) * Q_BLOCK))
    o_sel = o_sel.transpose(1, 0, 2, 3, 4, 5).reshape(B, S, H, dh)
    o_win = o_win.transpose(1, 0, 2, 3, 4, 5).reshape(B, S, H, dh)
    return gates[..., 0:1] * o_cmp + gates[..., 1:2] * o_sel + gates[..., 2:3] * o_win


def hybrid_layer(x, w_in, b_f, cmp_pos_k, cmp_pos_v, cmp_wk1, cmp_wk2, cmp_wv1, cmp_wv2,
                 w_a, w_b, w_o, ln_g, ln_b, rel_bias):
    B, S, _ = x.shape
    h = x @ w_in
    offsets = list(np.cumsum([c[1] for c in COL_LAYOUT])[:-1])
    (fq, fk, fv, ff, fz, nq, nkc, nvc, nks, nvs, nkw, nvw, ng, nz, ga, gb) = jnp.split(h, offsets, axis=-1)
    heads_a = lambda t: t.reshape(B, S, FOX_HEADS, HEAD_DIM)
    kv_b = lambda t: t.reshape(B, S, NSA_KV_GROUPS, HEAD_DIM)

    log_f = jax.nn.log_sigmoid((ff + b_f).astype(jnp.float32))
    o_a = fox_attention(heads_a(fq), heads_a(fk), heads_a(fv), log_f).reshape(B, S, FOX_WIDTH)
    y_a = (o_a * jax.nn.silu(fz)) @ w_a

    gates = jax.nn.sigmoid(ng.reshape(B, S, NSA_HEADS, N_NSA_BRANCHES))
    o_b = nsa_attention(nq.reshape(B, S, NSA_HEADS, HEAD_DIM), kv_b(nkc), kv_b(nvc), kv_b(nks), kv_b(nvs),
                        kv_b(nkw), kv_b(nvw), gates, cmp_pos_k, cmp_pos_v, cmp_wk1, cmp_wk2,
                        cmp_wv1, cmp_wv2, rel_bias).reshape(B, S, NSA_WIDTH)
    y_b = (o_b * jax.nn.silu(nz)) @ w_b

    merged = jax.nn.sigmoid(ga) * y_a + jax.nn.sigmoid(gb) * y_b
    return layer_norm(DEEPNORM_ALPHA * x + merged @ w_o, ln_g, ln_b)


def setup_inputs(seed: int = 0) -> dict:
    key = jax.random.key(seed)
    ks = jax.random.split(key, 16)
    f32 = jnp.float32
    col_scale = jnp.concatenate([jnp.full((c[1],), c[2], f32) for c in COL_LAYOUT])
    kdim = CMP_LEN * HEAD_DIM
    return {
        "x": jax.random.normal(ks[0], (BATCH, SEQ, D_MODEL), f32),
        "w_in": jax.random.normal(ks[1], (DEPTH, D_MODEL, IN_COLS), f32) * D_MODEL ** -0.5 * col_scale,
        "b_f": 3.0 + 0.1 * jax.random.normal(ks[2], (DEPTH, FOX_HEADS), f32),
        "cmp_pos_k": 0.1 * jax.random.normal(ks[3], (DEPTH, CMP_LEN, HEAD_DIM), f32),
        "cmp_pos_v": 0.1 * jax.random.normal(ks[4], (DEPTH, CMP_LEN, HEAD_DIM), f32),
        "cmp_wk1": jax.random.normal(ks[5], (DEPTH, kdim, CMP_HIDDEN), f32) * kdim ** -0.5,
        "cmp_wk2": jax.random.normal(ks[6], (DEPTH, CMP_HIDDEN, HEAD_DIM), f32) * CMP_HIDDEN ** -0.5,
        "cmp_wv1": jax.random.normal(ks[7], (DEPTH, kdim, CMP_HIDDEN), f32) * kdim ** -0.5,
        "cmp_wv2": jax.random.normal(ks[8], (DEPTH, CMP_HIDDEN, HEAD_DIM), f32) * CMP_HIDDEN ** -0.5,
        "w_a": jax.random.normal(ks[9], (DEPTH, FOX_WIDTH, D_MODEL), f32) * FOX_WIDTH ** -0.5 * DEEPNORM_BETA,
        "w_b": jax.random.normal(ks[10], (DEPTH, NSA_WIDTH, D_MODEL), f32) * NSA_WIDTH ** -0.5 * DEEPNORM_BETA,
        "w_o": jax.random.normal(ks[11], (DEPTH, D_MODEL, D_MODEL), f32) * D_MODEL ** -0.5 * DEEPNORM_BETA,
        "ln_g": 1.0 + 0.02 * jax.random.normal(ks[12], (DEPTH, D_MODEL), f32),
        "ln_b": 0.02 * jax.random.normal(ks[13], (DEPTH, D_MODEL), f32),
        "rel_bias": 0.5 * jax.random.normal(ks[14], (REL_BUCKETS, NSA_HEADS), f32),
    }


def reference(x, w_in, b_f, cmp_pos_k, cmp_pos_v, cmp_wk1, cmp_wk2, cmp_wv1, cmp_wv2,
              w_a, w_b, w_o, ln_g, ln_b, rel_bias):
    for layer in range(DEPTH):
        x = hybrid_layer(x, w_in[layer], b_f[layer], cmp_pos_k[layer], cmp_pos_v[layer],
                         cmp_wk1[layer], cmp_wk2[layer], cmp_wv1[layer], cmp_wv2[layer],
                         w_a[layer], w_b[layer], w_o[layer], ln_g[layer], ln_b[layer], rel_bias)
    return x
```

```python
import math
import contextlib
import numpy as np
import concourse.bass as bass
import concourse.mybir as mybir
from concourse.bass_utils import run_bass_kernel_spmd

F32 = mybir.dt.float32
BF16 = mybir.dt.bfloat16
AF = mybir.ActivationFunctionType
ALU = mybir.AluOpType
AX = mybir.AxisListType

ENGS = ("pe", "act", "dve", "pool", "sp")
SCALE = 128.0 ** -0.5
NSLAB = 93
NEGM = -3.0e5

OFF = dict(fq=0, fk=1024, fv=2048, ff=3072, fz=3080, nq=4104, kc=5128, vc=5384, ks=5640,
           vs=5896, kw=6152, vw=6408, ng=6664, nz=6688, ma=7712, mb=9760)

_CF = [("bf", 128), ("rb31", 8), ("bg", 8 * 5 * 128), ("mk", 5 * 128), ("cmv", 8 * 239),
       ("cmm", 239), ("vm", 256), ("fb", 256), ("ovl", 32), ("ident", 128), ("U", 128),
       ("lng", 2048), ("lnb", 2048)]
CO = {}
_o = 0
for _n, _w in _CF:
    CO[_n] = (_o, _w)
    _o += _w
NF = _o


class Prog:
    def __init__(self, nc, n_dma_sems=12):
        self.nc = nc
        self.ops = {e: [] for e in ENGS}
        self.seq = {e: 0 for e in ENGS}
        self.last_w = {}
        self.readers = {}
        self.waited = {e: {} for e in ENGS}
        self.n_dma = n_dma_sems
        self.dma_cnt = {}
        self.dma_rr = {e: 0 for e in ENGS}

    def _need(self, eng, tok, waits):
        if tok is None:
            return
        semkey, val, org = tok
        if org == eng and eng == "pe":
            return
        if self.waited[eng].get(semkey, 0) >= val:
            return
        waits[semkey] = max(waits.get(semkey, 0), val)

    def op(self, eng, fn, reads=(), writes=(), dma=False):
        waits = {}
        for k in reads:
            self._need(eng, self.last_w.get(k), waits)
        for k in writes:
            self._need(eng, self.last_w.get(k), waits)
            for t in self.readers.get(k, {}).values():
                self._need(eng, t, waits)
        if dma:
            si = self.dma_rr[eng]
            self.dma_rr[eng] = (si + 1) % self.n_dma
            semkey = ("dma", eng, si)
            prev = self.dma_cnt.get(semkey, 0)
            if prev > 0 and self.waited[eng].get(semkey, 0) < prev:
                waits[semkey] = max(waits.get(semkey, 0), prev)
            self.dma_cnt[semkey] = prev + 16
            tok = (semkey, prev + 16, None)
        else:
            self.seq[eng] += 1
            tok = (("eng", eng), self.seq[eng], eng)
        for sk, v in waits.items():
            self.waited[eng][sk] = max(self.waited[eng].get(sk, 0), v)
        self.ops[eng].append((fn, list(waits.items()), tok, dma))
        for k in reads:
            self.readers.setdefault(k, {})[tok[0]] = tok
        for k in writes:
            self.last_w[k] = tok
            self.readers[k] = {}
        return tok

    def emit(self):
        nc = self.nc
        with contextlib.ExitStack() as es:
            sems = {}
            for e in ENGS:
                sems[("eng", e)] = es.enter_context(nc.semaphore("s_" + e))
            for sk in self.dma_cnt:
                sems[sk] = es.enter_context(nc.semaphore("s_%s_%s%d" % sk))
            block = es.enter_context(nc.Block())

            def run(e, handle):
                for fn, waits, tok, dma in self.ops[e]:
                    for sk, v in waits:
                        handle.wait_ge(sems[sk], v)
                    ins = fn(handle)
                    ins.then_inc(sems[tok[0]], 16 if dma else 1)
                if e == "sp":
                    for sk, v in self.dma_cnt.items():
                        handle.wait_ge(sems[sk], v)
                    for e2 in ENGS:
                        if e2 != "sp" and self.seq[e2] > 0:
                            handle.wait_ge(sems[("eng", e2)], self.seq[e2])

            @block.tensor
            def _(h):
                run("pe", h)

            @block.scalar
            def _(h):
                run("act", h)

            @block.vector
            def _(h):
                run("dve", h)

            @block.gpsimd
            def _(h):
                run("pool", h)

            @block.sync
            def _(h):
                run("sp", h)


def build_program(debug=False, stop_after=None):
    nc = bass.Bass("TRN2", target_bir_lowering=False)

    def D(name, shape, dt=F32, kind="ExternalInput"):
        return nc.dram_tensor(name, shape, dt, kind=kind).ap()

    x_full = D("x_full", [2048, 2048])
    x_own = D("x_own", [1024, 2048])
    wr = D("wr", [128, NSLAB * 2048])
    w1d = D("w1", [128, 2 * 8192])
    w2d = D("w2", [128, 512])
    posd = D("posT", [128, 64])
    wad = D("wa", [128, 16 * 1024])
    wbd = D("wb", [128, 16 * 1024])
    wod = D("wo", [128, 16 * 2048])
    cst = D("cst", [128, NF])
    eexpd = D("eexp", [32, 2048])
    outd = D("out", [1024, 2048], kind="ExternalOutput")
    ozs = D("ozs", [2, 8, 128, 1024], BF16, kind=("ExternalOutput" if debug else "Internal"))

    def C(name):
        o, w = CO[name]
        return cst[:, o:o + w]

    P = Prog(nc)
    es = contextlib.ExitStack()
    with es:
        def SB(name, shape, dt):
            return es.enter_context(nc.sbuf_tensor(name, shape, dt))

        def PSUM(name, shape, dt):
            return es.enter_context(nc.psum_tensor(name, shape, dt))

        xTf = SB("xTf", [128, 16, 2048], BF16)
        xTo = SB("xTo", [128, 16, 1024], BF16)
        WB = SB("WB", [128, 6, 2048], BF16)
        kbuf = SB("kbuf", [128, 2, 2048], BF16)
        vbuf = SB("vbuf", [128, 2, 16, 129], BF16)
        qbuf = SB("qbuf", [128, 8, 1024], BF16)
        identb = SB("identb", [128, 128], BF16)
        Uf = SB("Uf", [128, 128], F32)
        onesf = SB("onesf", [128, 128], F32)
        bfb = SB("bfb", [128, 128], F32)
        rb31 = SB("rb31", [128, 8], F32)
        mkf = SB("mkf", [128, 640], F32)
        mkb = SB("mkb", [128, 2, 128], BF16)
        bgs = SB("bgs", [128, 2, 640], F32)
        EB = SB("EB", [128, 8, 640], BF16)
        cmb = SB("cmb", [128, 8, 239], F32)
        negm = SB("negm", [128, 239], F32)
        vmt = SB("vmt", [128, 8, 32], F32)
        fbt = SB("fbt", [128, 8, 32], F32)
        eexp = SB("eexp_sb", [32, 2048], BF16)
        gx = SB("gx", [128, 4, 128], F32)
        ffb, lfn, tots, cn = gx[:, 0, :], gx[:, 1, :], gx[:, 2, :], gx[:, 3, :]
        offn = SB("offn", [128, 128], F32)
        Bias = SB("Bias", [128, 8, 72], F32)
        PT = SB("PT", [128, 5, 4, 128], BF16)
        tmpf = SB("tmpf", [128, 2, 512], F32)
        G = SB("G", [128, 192], F32)
        onorm = SB("onorm", [128, 2, 128], BF16)
        rinv = SB("rinv", [128, 8], F32)
        acc0 = SB("acc", [128, 4, 128], F32)
        accb = SB("accb", [128, 2, 128], BF16)
        bgflat = bgs[:].rearrange("p a b -> p (a b)")
        csb4 = bgflat[:, 0:512].rearrange("p (a b) -> p a b", b=128)
        Ecm4 = bgflat[:, 512:768].bitcast(BF16).rearrange("p (a b) -> p a b", b=128)
        ETc4 = bgflat[:, 768:1024].bitcast(BF16).rearrange("p (a b) -> p a b", b=128)
        impa = SB("impa", [128, 2, 32], F32)
        imp2 = SB("imp2", [128, 32], F32)
        top8 = SB("top8", [128, 8], F32)
        nmk = SB("nmk", [128, 32], BF16)
        nmT = SB("nmT", [32, 2, 128], BF16)
        kcT = SB("kcT", [128, 2, 128], BF16)
        vcaug = SB("vcaug", [128, 2, 161], BF16)
        w2b = SB("w2b", [128, 4, 128], BF16)
        posb = SB("posb", [128, 64], BF16)
        cconst = SB("cconst", [128, 2], F32)
        gT = SB("gT", [128, 2, 128], BF16)

        psm = [PSUM("psm%d" % i, [128, 512], F32) for i in range(2)]
        pss = [PSUM("pss%d" % i, [128, 512], F32) for i in range(2)]
        pso = [PSUM("pso%d" % i, [128, 512], F32) for i in range(2)]
        tpp = [PSUM("tpp%d" % i, [128, 1024], BF16) for i in range(2)]

        sbank = [pss[0], pss[1], psm[0], psm[1]]
        sbkey = [("pss", 0), ("pss", 1), ("psm", 0), ("psm", 1)]
        NSB, NPT, LOOK = 4, 5, 3
        tpn = [2]
        cnt = {"ssf": 0, "mm": 0, "ss": 0, "oo": 0, "tp": 0, "wb": 0, "ev": 0, "pt": 0, "ab": 0, "tmp": 0,
               "on": 0, "cb": 0}

        def nxt(name, n):
            v = cnt[name]
            cnt[name] = (v + 1) % n
            return v

        def dma(eng, out_ap, in_ap, reads, writes):
            return P.op(eng, lambda h: h.dma_start(out=out_ap, in_=in_ap), reads, writes, dma=True)

        def mm(out_ap, lhsT, rhs, start, stop, reads, writes):
            return P.op("pe", lambda h: h.matmul(out_ap, lhsT, rhs, start=start, stop=stop),
                        reads, writes)

        def tr(out_ap, in_ap, reads, writes):
            return P.op("pe", lambda h: h.transpose(out_ap, in_ap, identb[:]),
                        list(reads) + ["ident"], writes)

        def act(out_ap, in_ap, func, reads, writes, bias=None, scale=None, accum=None):
            kw = {}
            if accum is not None:
                kw["accum_out"] = accum
            if bias is not None:
                kw["bias"] = bias
            if scale is not None:
                kw["scale"] = scale
            return P.op("act", lambda h: h.activation(out=out_ap, in_=in_ap, func=func, **kw),
                        reads, writes)

        def v_tt(out, in0, in1, op, reads, writes, eng="dve"):
            return P.op(eng, lambda h: h.tensor_tensor(out=out, in0=in0, in1=in1, op=op),
                        reads, writes)

        def v_ts(out, in0, s1, s2, op0, op1, reads, writes, eng="dve"):
            if s2 is None:
                return P.op(eng, lambda h: h.tensor_scalar(out=out, in0=in0, scalar1=s1,
                                                           scalar2=None, op0=op0), reads, writes)
            return P.op(eng, lambda h: h.tensor_scalar(out=out, in0=in0, scalar1=s1, scalar2=s2,
                                                       op0=op0, op1=op1), reads, writes)

        def v_stt(out, in0, scalar, in1, op0, op1, reads, writes, eng="dve"):
            return P.op(eng, lambda h: h.scalar_tensor_tensor(out=out, in0=in0, scalar=scalar,
                                                              in1=in1, op0=op0, op1=op1),
                        reads, writes)

        def v_copy(out, in_, reads, writes, eng="dve"):
            return P.op(eng, lambda h: h.tensor_copy(out=out, in_=in_), reads, writes)

        def v_recip(out, in_, reads, writes):
            return P.op("dve", lambda h: h.reciprocal(out=out, in_=in_), reads, writes)

        def v_memset(ap, val, writes, eng="pool"):
            return P.op(eng, lambda h: h.memset(ap, val), [], writes)

        def evac(out, in_, reads, writes):
            if nxt("ev", 2) == 0:
                return P.op("act", lambda h: h.copy(out=out, in_=in_), reads, writes)
            return v_copy(out, in_, reads, writes)

        pref = {}

        def prefetch_slabs(sis):
            for si in sis:
                if si in pref:
                    continue
                s = nxt("wb", 6)
                dma("pool", WB[:, s, :], wr[:, si * 2048:(si + 1) * 2048], [], [("wb", s)])
                pref[si] = s

        def load_slabs(si, n):
            prefetch_slabs(range(si, si + n))
            return [pref.pop(si + j) for j in range(n)]

        def xkeys(name, t0, t1):
            return [(name, b) for b in range(t0 // 128, (t1 + 127) // 128)]

        def proj_fm(slot, xT, xname, ntok, post):
            for tc in range(ntok // 512):
                b = nxt("mm", 2)
                ps = psm[b]
                bk = ("psm", b)
                rk = [("wb", slot)] + xkeys(xname, tc * 512, tc * 512 + 512)
                for k in range(16):
                    mm(ps[:, 0:512], WB[:, slot, k * 128:(k + 1) * 128],
                       xT[:, k, tc * 512:(tc + 1) * 512], k == 0, k == 15, rk, [bk])
                post(ps, bk, tc)

        def post_copy(dst, dkey):
            def f(ps, bk, tc):
                evac(dst[:, tc * 512:(tc + 1) * 512], ps[:, 0:512], [bk], [dkey])
            return f

        def post_silu(dst, dkey):
            def f(ps, bk, tc):
                tb = nxt("tmp", 2)
                t = tmpf[:, tb, :]
                act(t, ps[:, 0:512], AF.Exp, [bk], [("tmpf", tb)], scale=-1.0)
                v_ts(t, t, 1.0, None, ALU.add, None, [("tmpf", tb)], [("tmpf", tb)])
                v_recip(t, t, [("tmpf", tb)], [("tmpf", tb)])
                v_tt(dst[:, tc * 512:(tc + 1) * 512], ps[:, 0:512], t, ALU.mult,
                     [bk, ("tmpf", tb)], [dkey])
            return f

        def proj_v(slot, vi, vkey, tmp, tmpkeys):
            for tc in range(4):
                b = nxt("mm", 2)
                ps = psm[b]
                bk = ("psm", b)
                rk = [("wb", slot)] + xkeys("xTf", tc * 512, tc * 512 + 512)
                for k in range(16):
                    mm(ps[:, 0:512], WB[:, slot, k * 128:(k + 1) * 128],
                       xTf[:, k, tc * 512:(tc + 1) * 512], k == 0, k == 15, rk, [bk])
                evac(tmp[:, tc * 512:(tc + 1) * 512], ps[:, 0:512], [bk], tmpkeys)
            for half in range(2):
                ti = nxt("tp", tpn[0])
                for bb in range(8):
                    blk = half * 8 + bb
                    tr(tpp[ti][:, bb * 128:(bb + 1) * 128], tmp[:, blk * 128:(blk + 1) * 128],
                       tmpkeys, [("tp", ti)])
                evac(vbuf[:, vi, half * 8:(half + 1) * 8, 0:128],
                     tpp[ti][:, 0:1024].rearrange("p (b d) -> p b d", d=128), [("tp", ti)], [vkey])

        dma("pool", identb[:], C("ident"), [], ["ident"])
        dma("sp", Uf[:], C("U"), [], ["Uf"])
        v_memset(onesf[:], 1.0, ["onesf"])
        dma("sp", bfb[:], C("bf"), [], ["bfb"])
        dma("sp", rb31[:], C("rb31"), [], ["rb31"])
        dma("sp", mkf[:], C("mk"), [], ["mkf"])
        o_mk = CO["mk"][0]
        dma("sp", cmb[:].rearrange("p a b -> p (a b)"), C("cmv"), [], ["cmb"])
        dma("sp", negm[:], C("cmm"), [], ["negm"])
        dma("sp", vmt[:].rearrange("p a b -> p (a b)"), C("vm"), [], ["vmt"])
        dma("sp", fbt[:].rearrange("p a b -> p (a b)"), C("fb"), [], ["fbt"])
        dma("pool", eexp[:], eexpd, [], ["eexp"])
        dma("pool", w2b[:].rearrange("p a b -> p (a b)"), w2d, [], ["w2b"])
        dma("pool", posb[:], posd, [], ["posb"])
        for g in range(2):
            dma("pool", vcaug[:, g, 129:161], C("ovl"), [], [("vcaug", g)])
        v_memset(vcaug[:, :, 128:129], 1.0, [("vcaug", 0), ("vcaug", 1)])
        v_memset(vbuf[:, :, :, 128:129], 1.0, [("vbuf", 0), ("vbuf", 1)])
        if stop_after == "p0":
            P.emit()
            return nc
        def load_transposed(src, nblk, dstT, kname):
            for blk in range(nblk):
                s = nxt("wb", 6)
                dma("pool", WB[:, s, :], src[blk * 128:(blk + 1) * 128, :], [], [("wb", s)])
                for half in range(2):
                    ti = nxt("tp", tpn[0])
                    tp = tpp[ti]
                    for kk in range(8):
                        k = half * 8 + kk
                        tr(tp[:, kk * 128:(kk + 1) * 128], WB[:, s, k * 128:(k + 1) * 128],
                           [("wb", s)], [("tp", ti)])
                    evac(dstT[:, half * 8:(half + 1) * 8, blk * 128:(blk + 1) * 128],
                         tp[:, 0:1024].rearrange("p (k t) -> p k t", t=128), [("tp", ti)],
                         [(kname, blk)])

        load_transposed(x_full, 16, xTf, "xTf")
        load_transposed(x_own, 8, xTo, "xTo")

        if stop_after == "p1":
            P.emit()
            return nc
        (ms,) = load_slabs(32, 1)
        b = nxt("mm", 2)
        ps = psm[b]
        for blk in range(16):
            for k in range(16):
                mm(ps[:, blk * 8:(blk + 1) * 8], xTf[:, k, blk * 128:(blk + 1) * 128],
                   WB[:, ms, k * 128:k * 128 + 8], k == 0, k == 15,
                   [("wb", ms), ("xTf", blk)], [("psm", b)])
        v_tt(ffb, ps[:, 0:128], bfb[:], ALU.add, [("psm", b), "bfb"], ["gx0"])
        act(ffb, ffb, AF.Exp, ["gx0"], ["gx0"], scale=-1.0)
        act(lfn, ffb, AF.Ln, ["gx0"], ["gx1"], bias=1.0)
        b1 = nxt("mm", 2)
        mm(psm[b1][:, 0:128], Uf[:], lfn, True, True, ["Uf", "gx1"], [("psm", b1)])
        b2 = nxt("mm", 2)
        mm(psm[b2][:, 0:128], onesf[:], lfn, True, True, ["onesf", "gx1"], [("psm", b2)])
        v_copy(tots, psm[b2][:, 0:128], [("psm", b2)], ["gx2"])
        v_memset(offn[:, 0:8], 0.0, ["offn"], eng="dve")
        for j in range(1, 16):
            v_tt(offn[:, j * 8:(j + 1) * 8], offn[:, (j - 1) * 8:j * 8], tots[:, (j - 1) * 8:j * 8],
                 ALU.add, ["offn", "gx2"], ["offn"])
        v_tt(cn, psm[b1][:, 0:128], offn[:], ALU.add, [("psm", b1), "offn"], ["gx3"])
        cn3 = cn.rearrange("p (j h) -> p j h", h=8)
        for h in range(8):
            for m in range(4):
                nj = 4 * m + 4
                v_ts(Bias[:, h, 2 * m * (m + 1):2 * m * (m + 1) + nj], cn3[:, 0:nj, h],
                     offn[:, 4 * m * 8 + h:4 * m * 8 + h + 1], None, ALU.subtract, None,
                     ["gx3", "offn"], ["Bias"])

        v_ts(negm[:], negm[:], -1.0, 30000.0, ALU.add, ALU.mult, ["negm"], ["negm"])
        for h in range(8):
            v_tt(cmb[:, h, :], cmb[:, h, :], negm[:], ALU.add, ["cmb", "negm"], ["cmb"])
        o_bg = CO["bg"][0]
        v_ts(bgs[:, 1, :], mkf[:], -1.0, -NEGM, ALU.add, ALU.mult, ["mkf"], [("bgs", 1)])
        v_copy(mkb[:].rearrange("p a b -> p (a b)"), bgs[:, 1, 384:640], [("bgs", 1)], ["mkb"])
        for h in range(8):
            dma("sp", bgs[:, 0, :], cst[:, o_bg + h * 640:o_bg + (h + 1) * 640], [], [("bgs", 0)])
            v_ts(bgs[:, 0, :], bgs[:, 0, :], rb31[:, h:h + 1], 1.0 / SCALE, ALU.subtract, ALU.mult,
                 [("bgs", 0), "rb31"], [("bgs", 0)])
            v_tt(bgs[:, 0, :], bgs[:, 0, :], mkf[:], ALU.mult, [("bgs", 0), "mkf"], [("bgs", 0)])
            v_tt(EB[:, h, :], bgs[:, 0, :], bgs[:, 1, :], ALU.add, [("bgs", 0), ("bgs", 1)], [("EB", h)])

        if stop_after == "p2":
            P.emit()
            return nc
        def mm_bank(items, bkey):
            for n_, (o_, l_, r_, rk_) in enumerate(items):
                mm(o_, l_, r_, n_ == 0, n_ == len(items) - 1, rk_, [bkey])

        def run_tasks(tasks, side=(), look=None):
            side = list(side)
            look = LOOK if look is None else look
            pend = None
            for k in range(min(look, len(tasks))):
                tasks[k]["S"]()
            for k, t in enumerate(tasks):
                if k + look < len(tasks):
                    tasks[k + look]["S"]()
                t["A"]()
                t["V"]()
                for _ in range(-(-len(side) // (len(tasks) - k))):
                    side.pop(0)()
                if pend is not None:
                    pend()
                    pend = None
                if t.get("end"):
                    t["end"]()
                pend = t.get("late")
            if pend is not None:
                pend()

        def finalize_head(mix, h, i, src_bf, src_key, szT, szkey):
            ti = nxt("tp", tpn[0])
            tp = tpp[ti]
            tr(tp[:, 0:128], src_bf, [src_key], [("tp", ti)])
            ob = nxt("ab", 2)
            v_tt(accb[:, ob, :], tp[:, 0:128], szT[:, i * 128:(i + 1) * 128], ALU.mult,
                 [("tp", ti), szkey], [("accb", ob)])
            dma("sp", ozs[mix, h, :, i * 128:(i + 1) * 128], accb[:, ob, :], [("accb", ob)],
                [("ozs", mix)])

        _FH = 8
        for h in range(_FH):
            sq, sk, sv, sz = load_slabs(4 * h, 4)
            proj_fm(sk, xTf, "xTf", 2048, post_copy(kbuf[:, 0, :], ("kbuf", 0)))
            proj_v(sv, 0, ("vbuf", 0), kbuf[:, 1, :], [("kbuf", 1)])
            proj_fm(sq, xTo, "xTo", 1024, post_copy(qbuf[:, 0, :], ("qbuf", 0)))
            proj_fm(sz, xTo, "xTo", 1024, post_silu(qbuf[:, 1, :], ("qbuf", 1)))
            if h + 1 < _FH:
                prefetch_slabs(range(4 * h + 4, 4 * h + 8))
            else:
                prefetch_slabs([32, 33, 39])
            tasks = []
            PTf = PT[:].rearrange("p a b c -> p a (b c)")
            for m in range(4):
                i0, i1 = 2 * m, 2 * m + 1
                ob0 = nxt("on", 2)
                ob1 = nxt("on", 2)
                groups = [[(j, 256) for j in (jg, jg + 1)] for jg in range(0, 4 * m + 2, 2)]
                groups.append([(4 * m + 2, 128), (4 * m + 3, 128)])
                for gi, grp in enumerate(groups):
                    si = nxt("ssf", 4)
                    pb = nxt("pt", NPT)
                    half = grp[0][1] == 128
                    W = grp[0][1]

                    def S_fn(grp=grp, si=si, i0=i0, i1=i1, W=W, m=m, half=half):
                        q0 = (i0 if W == 256 else i1) * 128
                        items = []
                        for bi, (j, w) in enumerate(grp):
                            items.append((sbank[si][:, bi * W:(bi + 1) * W], kbuf[:, 0, j * 128:(j + 1) * 128],
                                          qbuf[:, 0, q0:q0 + W], [("kbuf", 0), ("qbuf", 0)]))
                        if half or grp[0][0] == 4 * m:
                            for bi in range(2):
                                items.append((sbank[si][:, bi * W:bi * W + 128], identb[:], mkb[:, bi, :],
                                              ["ident", "mkb"]))
                        mm_bank(items, sbkey[si])

                    def A_fn(grp=grp, si=si, pb=pb, m=m, h=h, W=W, half=half):
                        for bi, (j, w) in enumerate(grp):
                            idx = 2 * m * (m + 1) + j
                            act(PTf[:, pb, bi * W:(bi + 1) * W], sbank[si][:, bi * W:(bi + 1) * W], AF.Exp,
                                [sbkey[si], "Bias"], [("PT", pb, bi)], bias=Bias[:, h, idx:idx + 1],
                                scale=SCALE)

                    def V_fn(grp=grp, pb=pb, m=m, W=W, i0=i0, i1=i1):
                        for bi, (j, w) in enumerate(grp):
                            qs = [(0, 0, 4 * m + 1), (1, 1, 4 * m + 3)] if W == 256 else [(0, 1, 4 * m + 3)]
                            for (qq, oi, jlast) in qs:
                                mm(pso[oi][:, 0:129], PTf[:, pb, bi * W + qq * 128:bi * W + qq * 128 + 128],
                                   vbuf[:, 0, j, :], j == 0, j == jlast,
                                   [("PT", pb, bi), ("vbuf", 0)], [("pso", oi)])

                    t = dict(S=S_fn, A=A_fn, V=V_fn)
                    fin = None
                    if (not half) and grp[0][0] == 4 * m:
                        fin = (0, i0, ob0)
                    elif half:
                        fin = (1, i1, ob1)
                    if fin is not None:
                        def E_fn(fin=fin):
                            oi, i_, ob = fin
                            rc = 7 * oi
                            v_recip(rinv[:, rc:rc + 1], pso[oi][:, 128:129], [("pso", oi)], [("rinvf", oi)])
                            v_ts(onorm[:, ob, :], pso[oi][:, 0:128], rinv[:, rc:rc + 1], None, ALU.mult, None,
                                 [("pso", oi), ("rinvf", oi)], [("onorm", ob)])

                        def L_fn(fin=fin, h=h):
                            oi, i_, ob = fin
                            finalize_head(0, h, i_, onorm[:, ob, :], ("onorm", ob), qbuf[:, 1, :], ("qbuf", 1))
                        t["end"] = E_fn
                        t["late"] = L_fn
                    tasks.append(t)
            run_tasks(tasks, look=3)

        if stop_after == "fox":
            P.emit()
            return nc

        (ms,) = load_slabs(32, 1)
        b = nxt("mm", 2)
        ps = psm[b]
        for i in range(8):
            for k in range(16):
                mm(ps[:, i * 24:(i + 1) * 24], xTo[:, k, i * 128:(i + 1) * 128],
                   WB[:, ms, k * 128 + 8:k * 128 + 32], k == 0, k == 15,
                   [("wb", ms), ("xTo", i)], [("psm", b)])
        act(G[:], ps[:, 0:192], AF.Exp, [("psm", b)], ["G"], scale=-1.0)
        v_ts(G[:], G[:], 1.0, None, ALU.add, None, ["G"], ["G"])
        v_recip(G[:], G[:], ["G"], ["G"])

        for kv in range(2):
            sl = [load_slabs(33 + 6 * g + kv, 1)[0] for g in range(2)]
            for g in range(2):
                proj_fm(sl[g], xTf, "xTf", 2048, post_copy(kbuf[:, g, :], ("kbuf", g)))
            w1s = []
            for q in range(4):
                s = nxt("wb", 6)
                dma("pool", WB[:, s, :], w1d[:, kv * 8192 + q * 2048:kv * 8192 + (q + 1) * 2048],
                    [], [("wb", s)])
                w1s.append(s)

            def w1ap(l, hc):
                s = w1s[l // 8]
                o = (l % 8) * 256 + hc * 128
                return WB[:, s, o:o + 128], ("wb", s)

            for l in range(32):
                for hc in range(2):
                    wap, wk = w1ap(l, hc)
                    mm(pso[hc][:, 0:1], wap, posb[:, kv * 32 + l:kv * 32 + l + 1], l == 0, l == 31,
                       [wk, "posb"], [("pso", hc)])
            for hc in range(2):
                v_copy(cconst[:, hc:hc + 1], pso[hc][:, 0:1], [("pso", hc)], ["cconst"])
            for l in range(32):
                for g in range(2):
                    raw = kbuf[:, g, :].rearrange("p (c s) -> p c s", s=16)
                    rhs = raw[:, 0:127, l] if l < 16 else raw[:, 1:128, l - 16]
                    for hc in range(2):
                        wap, wk = w1ap(l, hc)
                        mm(sbank[2 * g + hc][:, 0:127], wap, rhs, l == 0, l == 31, [wk, ("kbuf", g)],
                           [sbkey[2 * g + hc]])
            for g in range(2):
                for hc in range(2):
                    ps = sbank[2 * g + hc]
                    hk = sbkey[2 * g + hc]
                    xs = gx[:, 0, 0:127]
                    x2 = gx[:, 1, 0:127]
                    v_ts(xs, ps[:, 0:127], cconst[:, hc:hc + 1], None, ALU.add, None,
                         [hk, "cconst"], ["gx0"])
                    v_tt(x2, xs, xs, ALU.mult, ["gx0"], ["gx1"])
                    v_ts(x2, x2, 0.044715, 1.0, ALU.mult, ALU.add, ["gx1"], ["gx1"])
                    v_tt(x2, x2, xs, ALU.mult, ["gx1", "gx0"], ["gx1"])
                    act(x2, x2, AF.Exp, ["gx1"], ["gx1"], scale=-1.5957691216057308)
                    v_ts(x2, x2, 1.0, None, ALU.add, None, ["gx1"], ["gx1"])
                    v_recip(x2, x2, ["gx1"], ["gx1"])
                    v_tt(gT[:, hc, 0:127], xs, x2, ALU.mult, ["gx0", "gx1"], [("gT", hc)])
                ps = pso[g]
                pk = ("pso", g)
                if kv == 0:
                    for hc in range(2):
                        mm(ps[:, 0:127], w2b[:, hc, :], gT[:, hc, 0:127], hc == 0, hc == 1,
                           ["w2b", ("gT", hc)], [pk])
                    evac(kcT[:, g, 0:127], ps[:, 0:127], [pk], [("kcT", g)])
                else:
                    for hc in range(2):
                        mm(ps[0:127, 0:128], gT[:, hc, 0:127], w2b[:, 2 + hc, :], hc == 0, hc == 1,
                           ["w2b", ("gT", hc)], [pk])
                    evac(vcaug[0:127, g, 0:128], ps[0:127, 0:128], [pk], [("vcaug", g)])

        allx = [("xTf", b_) for b_ in range(16)]
        xflat = xTf[:].rearrange("p k t -> p (k t)")
        mT = xflat[:, 0:16384].rearrange("p (c t) -> p c t", t=1024)
        ozT = xflat[:, 16384:32768].rearrange("p (m k t) -> p m k t", m=2, k=8)
        kbf = kbuf[:].rearrange("p a b -> p (a b)")
        vbf = vbuf[:].rearrange("p a b c -> p (a b c)")
        wbf = WB[:].rearrange("p a b -> p (a b)")
        wslb = [kbf[:, 0:2048], kbf[:, 2048:4096], vbf[:, 0:2048]]
        wslk = [[("kbuf", 0)], [("kbuf", 1)], [("vbuf", 0), ("vbuf", 1)]]
        wsgb = [wbf[:, 0:4096], wbf[:, 4096:8192], wbf[:, 8192:12288]]
        wsgk = [[("wb", 0), ("wb", 1)], [("wb", 2), ("wb", 3)], [("wb", 4), ("wb", 5)]]
        tpn[0] = 1
        cnt["tp"] = 0
        for g in range(2):
            s_ks, s_vs, s_kw, s_vw = load_slabs(33 + 6 * g + 2, 4)
            proj_fm(s_ks, xTf, "xTf", 2048, post_copy(kbuf[:, 0, :], ("kbuf", 0)))
            proj_v(s_vs, 0, ("vbuf", 0), kbuf[:, 1, :], [("kbuf", 1)])
            proj_v(s_vw, 1, ("vbuf", 1), qbuf[:, 0:2, :].rearrange("p a b -> p (a b)"),
                   [("qbuf", 0), ("qbuf", 1)])
            proj_fm(s_kw, xTf, "xTf", 2048, post_copy(kbuf[:, 1, :], ("kbuf", 1)))
            for hh in range(4):
                h = 4 * g + hh
                s_q, s_z = load_slabs(45 + 2 * h, 2)
                proj_fm(s_q, xTo, "xTo", 1024, post_copy(qbuf[:, hh, :], ("qbuf", hh)))
                proj_fm(s_z, xTo, "xTo", 1024, post_silu(qbuf[:, 4 + hh, :], ("qbuf", 4 + hh)))
            if g == 0:
                prefetch_slabs(range(33 + 6 + 2, 33 + 6 + 6))
            else:
                for cc in range(3):
                    dma("pool", wsgb[cc], wr[:, (61 + 2 * cc) * 2048:(63 + 2 * cc) * 2048], [],
                        [("wsg", cc)] + wsgk[cc])
                dma("sp", ozT[:, 0, :, :], ozs[0].rearrange("k p t -> p k t"), [("ozs", 0)],
                    [("ozT", 0)] + allx)
            accs = [acc0, gx]
            gxk = ["gx0", "gx1", "gx2", "gx3"]

            def cmp_sel_stages(i, g=g):
                ia = i % 2
                acc = accs[ia]
                CB = tpp[1][:].bitcast(F32)
                ck = ("tp", 1)
                w0 = 112 - 16 * i
                st = []

                def s1():
                    for hh in range(4):
                        mm(CB[:, hh * 128:hh * 128 + 127], qbuf[:, hh, i * 128:(i + 1) * 128], kcT[:, g, 0:127],
                           True, True, [("qbuf", hh), ("kcT", g)], [ck])
                st.append(s1)

                def s2():
                    for hh in range(4):
                        h = 4 * g + hh
                        v_stt(csb4[:, hh, 0:127], CB[:, hh * 128:hh * 128 + 127], SCALE, cmb[:, h, w0:w0 + 127],
                              ALU.mult, ALU.add, [ck, "cmb"], [("csb", hh), ("bgs", 0), ("bgs", 1)])
                        act(Ecm4[:, hh, 0:127], csb4[:, hh, 0:127], AF.Exp, [("csb", hh)], [("Ecm", hh)])
                st.append(s2)

                def s3():
                    ti = nxt("tp", tpn[0])
                    for hh in range(4):
                        tr(tpp[ti][0:127, hh * 128:(hh + 1) * 128], Ecm4[:, hh, 0:127], [("Ecm", hh)],
                           [("tp", ti)])
                    evac(ETc4[0:127, :, :], tpp[ti][0:127, 0:512].rearrange("p (a b) -> p a b", b=128),
                         [("tp", ti)], ["ETc"])
                st.append(s3)

                def oc(pair):
                    def f():
                        for hh in (2 * pair, 2 * pair + 1):
                            c0 = (hh % 2) * 256
                            mm(CB[:, c0:c0 + 161], ETc4[0:127, hh, :], vcaug[0:127, g, :], True, True,
                               ["ETc", ("vcaug", g)], [ck])
                    return f

                def dchain(pair):
                    def f():
                        for hh in (2 * pair, 2 * pair + 1):
                            h = 4 * g + hh
                            c0 = (hh % 2) * 256
                            v_ts(rinv[:, 1:2], CB[:, c0 + 128:c0 + 129], 1e-30, None, ALU.add, None, [ck],
                                 ["rinv1"])
                            v_recip(rinv[:, 1:2], rinv[:, 1:2], ["rinv1"], ["rinv1"])
                            v_tt(rinv[:, 2:3], rinv[:, 1:2], G[:, i * 24 + h * 3:i * 24 + h * 3 + 1], ALU.mult,
                                 ["rinv1", "G"], ["rinv2"])
                            v_ts(acc[:, hh, :], CB[:, c0:c0 + 128], rinv[:, 2:3], None, ALU.mult, None,
                                 [ck, "rinv2"], [("acc", ia, hh)] + (gxk if ia == 1 else []))
                            if hh == 0:
                                v_ts(impa[:, ia, :], CB[:, c0 + 129:c0 + 161], rinv[:, 1:2], None, ALU.mult,
                                     None, [ck, "rinv1"], [("impa", ia)])
                            else:
                                v_stt(impa[:, ia, :], CB[:, c0 + 129:c0 + 161], rinv[:, 1:2], impa[:, ia, :],
                                      ALU.mult, ALU.add, [ck, "rinv1", ("impa", ia)], [("impa", ia)])
                    return f
                st += [oc(0), dchain(0), oc(1), dchain(1)]

                def s6():
                    v_tt(imp2[:], impa[:, ia, :], vmt[:, i, :], ALU.mult, [("impa", ia), "vmt"], ["imp2"])
                    v_tt(imp2[:], imp2[:], fbt[:, i, :], ALU.add, ["imp2", "fbt"], ["imp2"])
                    P.op("dve", lambda hh_: hh_.max(out=top8[:], in_=imp2[:]), ["imp2"], ["top8"])
                    v_ts(nmk[:], imp2[:], top8[:, 7:8], NEGM, ALU.is_lt, ALU.mult, ["imp2", "top8"], ["nmk"])
                st.append(s6)

                def s7():
                    ti = nxt("tp", tpn[0])
                    tr(tpp[ti][0:32, 0:128], nmk[:], ["nmk"], [("tp", ti)])
                    evac(nmT[:, ia, :], tpp[ti][0:32, 0:128], [("tp", ti)], [("nmT", ia)])
                st.append(s7)
                return st

            for st_ in cmp_sel_stages(0):
                st_()
            for i in range(8):
                ia = i % 2
                acc = accs[ia]
                tasks = []
                for hh in range(4):
                    h = 4 * g + hh
                    nj = 2 * i + 2
                    oi = nxt("oo", 2)
                    for jg in range(0, nj, 4):
                        n = min(4, nj - jg)
                        si = nxt("ss", NSB)
                        pb = nxt("pt", NPT)

                        def S_fn(i=i, jg=jg, n=n, si=si, hh=hh, ia=ia):
                            qa = qbuf[:, hh, i * 128:(i + 1) * 128]
                            h = 4 * g + hh
                            items = []
                            for jj in range(n):
                                j = jg + jj
                                items.append((sbank[si][:, jj * 128:(jj + 1) * 128],
                                              kbuf[:, 0, j * 128:(j + 1) * 128], qa, [("kbuf", 0), ("qbuf", hh)]))
                            for jj in range(n):
                                j = jg + jj
                                items.append((sbank[si][:, jj * 128:(jj + 1) * 128],
                                              eexp[:, j * 128:(j + 1) * 128], nmT[:, ia, :],
                                              ["eexp", ("nmT", ia)]))
                            for jj in range(n):
                                r = jg + jj - (2 * i - 4)
                                if r >= 3:
                                    q = r - 1
                                    items.append((sbank[si][:, jj * 128:(jj + 1) * 128], identb[:],
                                                  EB[:, h, q * 128:(q + 1) * 128], ["ident", ("EB", h)]))
                            mm_bank(items, sbkey[si])

                        def A_fn(i=i, jg=jg, n=n, si=si, pb=pb, h=h):
                            act(PT[:, pb, 0:n, :], sbank[si][:, 0:n * 128].rearrange("p (a b) -> p a b", b=128),
                                AF.Exp, [sbkey[si]], [("PT", pb, jj) for jj in range(n)],
                                bias=rb31[:, h:h + 1], scale=SCALE)

                        def V_fn(jg=jg, n=n, pb=pb, oi=oi, nj=nj):
                            for jj in range(n):
                                j = jg + jj
                                mm(pso[oi][:, 0:129], PT[:, pb, jj, :], vbuf[:, 0, j, :], j == 0, j == nj - 1,
                                   [("PT", pb, jj), ("vbuf", 0)], [("pso", oi)])

                        t = dict(S=S_fn, A=A_fn, V=V_fn)
                        if jg + n == nj:
                            def E_fn(oi=oi, hh=hh, h=h, i=i, acc=acc, ia=ia):
                                v_recip(rinv[:, 3:4], pso[oi][:, 128:129], [("pso", oi)], ["rinv3"])
                                v_tt(rinv[:, 4:5], rinv[:, 3:4], G[:, i * 24 + h * 3 + 1:i * 24 + h * 3 + 2],
                                     ALU.mult, ["rinv3", "G"], ["rinv4"])
                                v_stt(acc[:, hh, :], pso[oi][:, 0:128], rinv[:, 4:5], acc[:, hh, :], ALU.mult,
                                      ALU.add, [("pso", oi), "rinv4", ("acc", ia, hh)], [("acc", ia, hh)])
                            t["end"] = E_fn
                        tasks.append(t)
                    j0 = max(0, 2 * i - 4)
                    js = list(range(j0, 2 * i + 2))
                    oi = nxt("oo", 2)
                    ob = nxt("on", 2)
                    for c0 in range(0, len(js), 4):
                        grp = js[c0:c0 + 4]
                        n = len(grp)
                        si = nxt("ss", NSB)
                        pb = nxt("pt", NPT)

                        def S_fn(i=i, grp=grp, si=si, hh=hh):
                            qa = qbuf[:, hh, i * 128:(i + 1) * 128]
                            h = 4 * g + hh
                            items = []
                            for jj, j in enumerate(grp):
                                items.append((sbank[si][:, jj * 128:(jj + 1) * 128],
                                              kbuf[:, 1, j * 128:(j + 1) * 128], qa, [("kbuf", 1), ("qbuf", hh)]))
                            for jj, j in enumerate(grp):
                                r = j - (2 * i - 4)
                                if r != 2:
                                    q = r if r < 2 else r - 1
                                    items.append((sbank[si][:, jj * 128:(jj + 1) * 128], identb[:],
                                                  EB[:, h, q * 128:(q + 1) * 128], ["ident", ("EB", h)]))
                            mm_bank(items, sbkey[si])

                        def A_fn(i=i, grp=grp, n=n, si=si, pb=pb, h=h):
                            act(PT[:, pb, 0:n, :], sbank[si][:, 0:n * 128].rearrange("p (a b) -> p a b", b=128),
                                AF.Exp, [sbkey[si]], [("PT", pb, jj) for jj in range(n)],
                                bias=rb31[:, h:h + 1], scale=SCALE)

                        def V_fn(grp=grp, pb=pb, oi=oi, js=js):
                            for jj, j in enumerate(grp):
                                mm(pso[oi][:, 0:129], PT[:, pb, jj, :], vbuf[:, 1, j, :], j == js[0], j == js[-1],
                                   [("PT", pb, jj), ("vbuf", 1)], [("pso", oi)])

                        t = dict(S=S_fn, A=A_fn, V=V_fn)
                        if grp[-1] == js[-1]:
                            def E_fn(oi=oi, ob=ob, hh=hh, h=h, i=i, acc=acc, ia=ia):
                                v_recip(rinv[:, 5:6], pso[oi][:, 128:129], [("pso", oi)], ["rinv5"])
                                v_tt(rinv[:, 6:7], rinv[:, 5:6], G[:, i * 24 + h * 3 + 2:i * 24 + h * 3 + 3],
                                     ALU.mult, ["rinv5", "G"], ["rinv6"])
                                v_stt(onorm[:, ob, :], pso[oi][:, 0:128], rinv[:, 6:7], acc[:, hh, :], ALU.mult,
                                      ALU.add, [("pso", oi), "rinv6", ("acc", ia, hh)], [("onorm", ob)])

                            def L_fn(i=i, ob=ob, h=h, hh=hh):
                                finalize_head(1, h, i, onorm[:, ob, :], ("onorm", ob), qbuf[:, 4 + hh, :],
                                              ("qbuf", 4 + hh))
                            t["end"] = E_fn
                            t["late"] = L_fn
                        tasks.append(t)
                run_tasks(tasks, side=(cmp_sel_stages(i + 1) if i + 1 < 8 else ()))

        tpn[0] = 2
        if stop_after == "nsa":
            P.emit()
            return nc

        dma("sp", ozT[:, 1, :, :], ozs[1].rearrange("k p t -> p k t"), [("ozs", 1)],
            [("ozT", 1)] + allx)
        ebt = EB[:].rearrange("p a b -> p (a b)").bitcast(F32)
        tm2 = [[tmpf[:, 0, :], tmpf[:, 1, :]], [ebt[:, 0:512], ebt[:, 512:1024]]]
        tk2 = [[("tmpf", 0), ("tmpf", 1)], [("ebt", 0), ("ebt", 1)]]
        ebk = [("EB", h_) for h_ in range(8)]
        tpf = [tpp[0][:].bitcast(F32), tpp[1][:].bitcast(F32)]
        bsets = [[(psm[0], ("psm", 0)), (psm[1], ("psm", 1)), (pss[0], ("pss", 0)), (pss[1], ("pss", 1))],
                 [(pso[0], ("pso", 0)), (pso[1], ("pso", 1)), (tpf[0], ("tp", 0)), (tpf[1], ("tp", 1))]]
        woA = xTo[:].rearrange("p k t -> p (k t)")
        woB1 = qbuf[:].rearrange("p a b -> p (a b)")
        woB2 = vbf[:, 2048:4096]
        woB3 = wbf[:, 0:6144]
        step = 0
        for cc in range(16):
            pb2 = cc % 3
            wsl, wsg = wslb[pb2], wsgb[pb2]
            if cc >= 3:
                dma("pool", wsg, wr[:, (61 + 2 * cc) * 2048:(63 + 2 * cc) * 2048], [], [("wsg", pb2)])
            dma("pool", wsl[:, 0:1024], wad[:, cc * 1024:(cc + 1) * 1024], [],
                [("wsl", pb2, 0)] + (wslk[pb2] if cc < 3 else []))
            dma("pool", wsl[:, 1024:2048], wbd[:, cc * 1024:(cc + 1) * 1024],
                [], [("wsl", pb2, 1)])
            if cc == 2:
                dma("pool", woB1, wod[:, 16384:24576], [], ["woB1"] + [("qbuf", q_) for q_ in range(8)])
                dma("pool", woB2, wod[:, 24576:26624], [], ["woB2"])
            for th in range(2):
                tsl = slice(th * 512, (th + 1) * 512)
                bs = bsets[step % 2]
                tms, tks = tm2[step % 2], tk2[step % 2]
                step += 1
                for mix in range(2):
                    bank, bk = bs[2 + mix]
                    for k in range(16):
                        o = mix * 2048 + k * 128
                        mm(bank[:, 0:512], wsg[:, o:o + 128], xTo[:, k, tsl], k == 0, k == 15,
                           [("wsg", pb2)] + xkeys("xTo", th * 512, th * 512 + 512), [bk])
                for mix in range(2):
                    bank, bk = bs[mix]
                    for k in range(8):
                        o = mix * 1024 + k * 128
                        mm(bank[:, 0:512], wsl[:, o:o + 128], ozT[:, mix, k, tsl], k == 0, k == 7,
                           [("wsl", pb2, mix), ("ozT", mix)], [bk])
                for mix in range(2):
                    act(tms[mix], bs[2 + mix][0][:, 0:512], AF.Sigmoid, [bs[2 + mix][1]], [tks[mix]] + ebk)
                v_tt(tms[0], tms[0], bs[0][0][:, 0:512], ALU.mult, [tks[0], bs[0][1]], [tks[0]])
                v_tt(tms[1], tms[1], bs[1][0][:, 0:512], ALU.mult, [tks[1], bs[1][1]], [tks[1]])
                v_tt(mT[:, cc, tsl], tms[0], tms[1], ALU.add, [tks[0], tks[1]],
                     [("mT", cc)] + allx)
        f32v = xflat[:, 16384:32768].bitcast(F32)
        lng = f32v[:, 0:2048]
        lnb = f32v[:, 2048:4096]
        xr = [f32v[:, 4096:6144], kbf.bitcast(F32)]
        yo = [f32v[:, 6144:8192], wbf[:, 6144:10240].bitcast(F32)]
        dma("sp", lng, C("lng"), [], ["lng", ("ozT", 0), ("ozT", 1)])
        dma("sp", lnb, C("lnb"), [], ["lnb"])
        xo_keys = [("xTo", b_) for b_ in range(8)]
        dma("pool", woB3, wod[:, 26624:32768], [], ["woB3"] + [("wsg", q_) for q_ in range(3)])
        dma("pool", woA, wod[:, 0:16384], [], ["woA"] + xo_keys)

        def wo_ap(k, c0):
            if k < 8:
                return woA[:, k * 2048 + c0:k * 2048 + c0 + 512], "woA"
            if k < 12:
                return woB1[:, (k - 8) * 2048 + c0:(k - 8) * 2048 + c0 + 512], "woB1"
            if k < 13:
                return woB2[:, c0:c0 + 512], "woB2"
            return woB3[:, (k - 13) * 2048 + c0:(k - 13) * 2048 + c0 + 512], "woB3"

        alpha = 2.0 ** 0.25
        for i in range(8):
            u = i % 2
            xres, yout = xr[u], yo[u]
            xk, yk = ("xres", u), ("yout", u)
            first = ([("wsg", q_) for q_ in range(3)] + [("wsl", q_, m_) for q_ in range(3) for m_ in range(2)]) \
                if i == 1 else []
            dma("sp", xres, x_own[i * 128:(i + 1) * 128, :], [], [xk] + first)
            bs = bsets[i % 2]
            korder = list(range(8, 13)) + list(range(13, 16)) + list(range(8))
            for cg in range(4):
                bank, bk = bs[cg]
                for kn, k in enumerate(korder):
                    wap, wk = wo_ap(k, cg * 512)
                    mm(bank[:, 0:512], mT[:, k, i * 128:(i + 1) * 128], wap, kn == 0, kn == 15,
                       [("mT", k), wk], [bk])
                v_stt(xres[:, cg * 512:(cg + 1) * 512], xres[:, cg * 512:(cg + 1) * 512], alpha,
                      bank[:, 0:512], ALU.mult, ALU.add, [xk, bk], [xk])
            c0_ = 4 * u
            s1, s2, nm_, t_ = (rinv[:, c0_:c0_ + 1], rinv[:, c0_ + 1:c0_ + 2], rinv[:, c0_ + 2:c0_ + 3],
                               rinv[:, c0_ + 3:c0_ + 4])
            lk = ("ln", u)
            act(yout, xres, AF.Identity, [xk], [yk, lk] + first, accum=s1)
            act(yout, xres, AF.Square, [xk], [yk, lk], accum=s2)
            v_ts(nm_, s1, -1.0 / 2048.0, None, ALU.mult, None, [lk], [lk])
            v_tt(t_, nm_, nm_, ALU.mult, [lk], [lk])
            v_stt(s2, s2, 1.0 / 2048.0, t_, ALU.mult, ALU.subtract, [lk], [lk])
            act(s2, s2, AF.Sqrt, [lk], [lk], bias=1e-5)
            v_recip(s2, s2, [lk], [lk])
            v_tt(nm_, nm_, s2, ALU.mult, [lk], [lk])
            act(yout, xres, AF.Identity, [xk, lk], [yk], bias=nm_, scale=s2)
            v_tt(yout, yout, lng, ALU.mult, [yk, "lng"], [yk])
            v_tt(yout, yout, lnb, ALU.add, [yk, "lnb"], [yk], eng="pool")
            dma("pool", outd[i * 128:(i + 1) * 128, :], yout, [yk], [("outd", i)])
        P.emit()
    return nc


def _rel_bucket(dist):
    n = np.maximum(dist, 0)
    exact = 16
    nf = np.maximum(n, 1).astype(np.float32)
    large = exact + (np.log(nf / np.float32(exact)) / np.float32(math.log(128 / 16))
                     * np.float32(16)).astype(np.int32)
    return np.where(n < exact, n, np.minimum(large, 31)).astype(np.int64)


def _shared_arrays(w_in, cmp_wk1, cmp_wk2, cmp_wv1, cmp_wv2, cmp_pos_k, cmp_pos_v, w_a, w_b, w_o):
    W = np.asarray(w_in[0], dtype=np.float32)
    cols = []
    for h in range(8):
        for nm in ("fq", "fk", "fv", "fz"):
            cols.append(np.arange(OFF[nm] + 128 * h, OFF[nm] + 128 * h + 128))
    cols.append(np.concatenate([np.arange(OFF["ff"], OFF["ff"] + 8), np.arange(OFF["ng"], OFF["ng"] + 24),
                                np.full(96, -1)]))
    for g in range(2):
        for nm in ("kc", "vc", "ks", "vs", "kw", "vw"):
            cols.append(np.arange(OFF[nm] + 128 * g, OFF[nm] + 128 * g + 128))
    for h in range(8):
        for nm in ("nq", "nz"):
            cols.append(np.arange(OFF[nm] + 128 * h, OFF[nm] + 128 * h + 128))
    for cc in range(16):
        for nm in ("ma", "mb"):
            cols.append(np.arange(OFF[nm] + 128 * cc, OFF[nm] + 128 * cc + 128))
    assert len(cols) == NSLAB
    cols = np.concatenate(cols)
    Wz = np.concatenate([W, np.zeros((2048, 1), np.float32)], axis=1)
    Wp = Wz[:, cols]
    wr = np.ascontiguousarray(Wp.reshape(16, 128, NSLAB, 128).transpose(1, 2, 0, 3)).reshape(128, -1)

    def w1l(w):
        return np.asarray(w[0], np.float32).reshape(32, 128, 256).transpose(1, 0, 2).reshape(128, 8192)

    w1 = np.ascontiguousarray(np.concatenate([w1l(cmp_wk1), w1l(cmp_wv1)], axis=1))

    def w2l(w):
        return np.asarray(w[0], np.float32).reshape(2, 128, 128).transpose(1, 0, 2).reshape(128, 256)

    w2 = np.ascontiguousarray(np.concatenate([w2l(cmp_wk2), w2l(cmp_wv2)], axis=1))
    posT = np.ascontiguousarray(np.concatenate([np.asarray(cmp_pos_k[0], np.float32).T,
                                                np.asarray(cmp_pos_v[0], np.float32).T], axis=1))

    def wab(w):
        return np.ascontiguousarray(np.asarray(w[0], np.float32).reshape(8, 128, 16, 128)
                                    .transpose(1, 2, 0, 3)).reshape(128, -1)

    wo = np.ascontiguousarray(np.asarray(w_o[0], np.float32).reshape(16, 128, 2048)
                              .transpose(1, 0, 2)).reshape(128, -1)
    eexp = (np.arange(2048)[None, :] // 64 == np.arange(32)[:, None]).astype(np.float32)
    return dict(wr=wr, w1=w1, w2=w2, posT=posT, wa=wab(w_a), wb=wab(w_b), wo=wo, eexp=eexp)


def _const_pack(z, b_f, rel_bias, ln_g, ln_b):
    rb = np.asarray(rel_bias, np.float32)
    cp = np.zeros((128, NF), np.float32)

    def put(name, arr):
        o, w = CO[name]
        cp[:, o:o + w] = np.asarray(arr, np.float32).reshape(128, w)

    put("bf", np.tile(np.asarray(b_f[0], np.float32), (128, 16)))
    put("rb31", np.tile(rb[31], (128, 1)))
    sl = np.arange(128)[:, None]
    tl = np.arange(128)[None, :]
    bg = np.zeros((128, 8, 5, 128), np.float32)
    mk = np.zeros((128, 5, 128), np.float32)
    for q, r in enumerate((0, 1, 3, 4, 5)):
        dist = 128 * (z + 4 - r) + tl - sl
        valid = (dist >= 0) & (dist < 512)
        bk = _rel_bucket(dist)
        mk[:, q, :] = valid
        for h in range(8):
            bg[:, h, q, :] = np.where(valid, rb[bk, h], 0.0)
    put("bg", bg)
    put("mk", mk)
    tlc = np.arange(128)[:, None]
    w = np.arange(239)[None, :]
    dist = 128 * z + tlc - 16 * (w - 112) - 31
    valid = dist >= 0
    bk = _rel_bucket(dist)
    cmv = np.zeros((128, 8, 239), np.float32)
    for h in range(8):
        cmv[:, h, :] = np.where(valid, rb[bk, h], 0.0)
    put("cmv", cmv)
    put("cmm", valid.astype(np.float32))
    vm = np.zeros((128, 8, 32), np.float32)
    fb = np.zeros((128, 8, 32), np.float32)
    j = np.arange(32)[None, :]
    for i in range(8):
        t = 128 * (2 * i + z) + np.arange(128)[:, None]
        cur = t // 64
        forced = (j == 0) | (j == cur) | (j == cur - 1)
        val = (64 * j) <= t
        vm[:, i, :] = val & ~forced
        fb[:, i, :] = np.where(val, np.where(forced, 1e30, 0.0), -1e30)
    put("vm", vm)
    put("fb", fb)
    c = np.arange(128)[:, None]
    ovl = ((16 * c < 64 * j + 64) & (16 * c + 32 > 64 * j) & (c < 127)).astype(np.float32)
    put("ovl", ovl)
    put("ident", np.eye(128, dtype=np.float32))
    put("U", (np.arange(128)[:, None] <= np.arange(128)[None, :]).astype(np.float32))
    put("lng", np.tile(np.asarray(ln_g[0], np.float32), (128, 1)))
    put("lnb", np.tile(np.asarray(ln_b[0], np.float32), (128, 1)))
    return cp


def _in_maps(x, shared, b_f, rel_bias, ln_g, ln_b):
    x = np.asarray(x, np.float32)
    packs = [_const_pack(z, b_f, rel_bias, ln_g, ln_b) for z in range(2)]
    maps = []
    for c in range(8):
        b, z = c // 2, c % 2
        xb = x[b]
        xo = np.ascontiguousarray(xb.reshape(8, 2, 128, 2048)[:, z].reshape(1024, 2048))
        m = dict(shared)
        m["x_full"] = np.ascontiguousarray(xb)
        m["x_own"] = xo
        m["cst"] = packs[z]
        maps.append(m)
    return maps


_NC_CACHE = {}


def kernel(x, w_in, b_f, cmp_pos_k, cmp_pos_v, cmp_wk1, cmp_wk2, cmp_wv1, cmp_wv2,
           w_a, w_b, w_o, ln_g, ln_b, rel_bias):
    shared = _shared_arrays(w_in, cmp_wk1, cmp_wk2, cmp_wv1, cmp_wv2, cmp_pos_k, cmp_pos_v,
                            w_a, w_b, w_o)
    maps = _in_maps(x, shared, b_f, rel_bias, ln_g, ln_b)
    nc = build_program()
    res = run_bass_kernel_spmd(nc, maps, core_ids=list(range(8)))
    out = np.zeros((4, 2048, 2048), np.float32)
    for c in range(8):
        b, z = c // 2, c % 2
        out[b].reshape(8, 2, 128, 2048)[:, z] = np.asarray(res.results[c]["out"]).reshape(8, 128, 2048)
    return out
```

```python
import math
import contextlib
import numpy as np
import concourse.bass as bass
import concourse.mybir as mybir
from concourse.bass_utils import run_bass_kernel_spmd

F32 = mybir.dt.float32
BF16 = mybir.dt.bfloat16
AF = mybir.ActivationFunctionType
ALU = mybir.AluOpType
AX = mybir.AxisListType

ENGS = ("pe", "act", "dve", "pool", "sp")
SCALE = 128.0 ** -0.5
NSLAB = 93
NEGM = -3.0e5

OFF = dict(fq=0, fk=1024, fv=2048, ff=3072, fz=3080, nq=4104, kc=5128, vc=5384, ks=5640,
           vs=5896, kw=6152, vw=6408, ng=6664, nz=6688, ma=7712, mb=9760)

_CF = [("bf", 128), ("rb31", 8), ("bg", 8 * 5 * 128), ("mk", 5 * 128), ("cmv", 8 * 239),
       ("cmm", 239), ("vm", 256), ("fb", 256), ("ovl", 32), ("ident", 128), ("U", 128),
       ("lng", 2048), ("lnb", 2048)]
CO = {}
_o = 0
for _n, _w in _CF:
    CO[_n] = (_o, _w)
    _o += _w
NF = _o


class Prog:
    def __init__(self, nc, n_dma_sems=12):
        self.nc = nc
        self.ops = {e: [] for e in ENGS}
        self.seq = {e: 0 for e in ENGS}
        self.last_w = {}
        self.readers = {}
        self.waited = {e: {} for e in ENGS}
        self.n_dma = n_dma_sems
        self.dma_cnt = {}
        self.dma_rr = {e: 0 for e in ENGS}

    def _need(self, eng, tok, waits):
        if tok is None:
            return
        semkey, val, org = tok
        if org == eng and eng == "pe":
            return
        if self.waited[eng].get(semkey, 0) >= val:
            return
        waits[semkey] = max(waits.get(semkey, 0), val)

    def op(self, eng, fn, reads=(), writes=(), dma=False):
        waits = {}
        for k in reads:
            self._need(eng, self.last_w.get(k), waits)
        for k in writes:
            self._need(eng, self.last_w.get(k), waits)
            for t in self.readers.get(k, {}).values():
                self._need(eng, t, waits)
        if dma:
            si = self.dma_rr[eng]
            self.dma_rr[eng] = (si + 1) % self.n_dma
            semkey = ("dma", eng, si)
            prev = self.dma_cnt.get(semkey, 0)
            if prev > 0 and self.waited[eng].get(semkey, 0) < prev:
                waits[semkey] = max(waits.get(semkey, 0), prev)
            self.dma_cnt[semkey] = prev + 16
            tok = (semkey, prev + 16, None)
        else:
            self.seq[eng] += 1
            tok = (("eng", eng), self.seq[eng], eng)
        for sk, v in waits.items():
            self.waited[eng][sk] = max(self.waited[eng].get(sk, 0), v)
        self.ops[eng].append((fn, list(waits.items()), tok, dma))
        for k in reads:
            self.readers.setdefault(k, {})[tok[0]] = tok
        for k in writes:
            self.last_w[k] = tok
            self.readers[k] = {}
        return tok

    def emit(self):
        nc = self.nc
        with contextlib.ExitStack() as es:
            sems = {}
            for e in ENGS:
                sems[("eng", e)] = es.enter_context(nc.semaphore("s_" + e))
            for sk in self.dma_cnt:
                sems[sk] = es.enter_context(nc.semaphore("s_%s_%s%d" % sk))
            block = es.enter_context(nc.Block())

            def run(e, handle):
                for fn, waits, tok, dma in self.ops[e]:
                    for sk, v in waits:
                        handle.wait_ge(sems[sk], v)
                    ins = fn(handle)
                    ins.then_inc(sems[tok[0]], 16 if dma else 1)
                if e == "sp":
                    for sk, v in self.dma_cnt.items():
                        handle.wait_ge(sems[sk], v)
                    for e2 in ENGS:
                        if e2 != "sp" and self.seq[e2] > 0:
                            handle.wait_ge(sems[("eng", e2)], self.seq[e2])

            @block.tensor
            def _(h):
                run("pe", h)

            @block.scalar
            def _(h):
                run("act", h)

            @block.vector
            def _(h):
                run("dve", h)

            @block.gpsimd
            def _(h):
                run("pool", h)

            @block.sync
            def _(h):
                run("sp", h)


def build_program(debug=False, stop_after=None):
    nc = bass.Bass("TRN2", target_bir_lowering=False)

    def D(name, shape, dt=F32, kind="ExternalInput"):
        return nc.dram_tensor(name, shape, dt, kind=kind).ap()

    x_full = D("x_full", [2048, 2048])
    x_own = D("x_own", [1024, 2048])
    wr = D("wr", [128, NSLAB * 2048])
    w1d = D("w1", [128, 2 * 8192])
    w2d = D("w2", [128, 512])
    posd = D("posT", [128, 64])
    wad = D("wa", [128, 16 * 1024])
    wbd = D("wb", [128, 16 * 1024])
    wod = D("wo", [128, 16 * 2048])
    cst = D("cst", [128, NF])
    eexpd = D("eexp", [32, 2048])
    outd = D("out", [1024, 2048], kind="ExternalOutput")
    ozs = D("ozs", [2, 8, 128, 1024], BF16, kind=("ExternalOutput" if debug else "Internal"))

    def C(name):
        o, w = CO[name]
        return cst[:, o:o + w]

    P = Prog(nc)
    es = contextlib.ExitStack()
    with es:
        def SB(name, shape, dt):
            return es.enter_context(nc.sbuf_tensor(name, shape, dt))

        def PSUM(name, shape, dt):
            return es.enter_context(nc.psum_tensor(name, shape, dt))

        xTf = SB("xTf", [128, 16, 2048], BF16)
        xTo = SB("xTo", [128, 16, 1024], BF16)
        WB = SB("WB", [128, 6, 2048], BF16)
        kbuf = SB("kbuf", [128, 2, 2048], BF16)
        vbuf = SB("vbuf", [128, 2, 16, 129], BF16)
        qbuf = SB("qbuf", [128, 8, 1024], BF16)
        identb = SB("identb", [128, 128], BF16)
        Uf = SB("Uf", [128, 128], F32)
        onesf = SB("onesf", [128, 128], F32)
        bfb = SB("bfb", [128, 128], F32)
        rb31 = SB("rb31", [128, 8], F32)
        mkf = SB("mkf", [128, 640], F32)
        mkb = SB("mkb", [128, 2, 128], BF16)
        bgs = SB("bgs", [128, 2, 640], F32)
        EB = SB("EB", [128, 8, 640], BF16)
        cmb = SB("cmb", [128, 8, 239], F32)
        negm = SB("negm", [128, 239], F32)
        vmt = SB("vmt", [128, 8, 32], F32)
        fbt = SB("fbt", [128, 8, 32], F32)
        eexp = SB("eexp_sb", [32, 2048], BF16)
        gx = SB("gx", [128, 4, 128], F32)
        ffb, lfn, tots, cn = gx[:, 0, :], gx[:, 1, :], gx[:, 2, :], gx[:, 3, :]
        offn = SB("offn", [128, 128], F32)
        Bias = SB("Bias", [128, 8, 72], F32)
        PT = SB("PT", [128, 5, 4, 128], BF16)
        tmpf = SB("tmpf", [128, 2, 512], F32)
        G = SB("G", [128, 192], F32)
        onorm = SB("onorm", [128, 2, 128], BF16)
        rinv = SB("rinv", [128, 8], F32)
        acc0 = SB("acc", [128, 4, 128], F32)
        accb = SB("accb", [128, 2, 128], BF16)
        bgflat = bgs[:].rearrange("p a b -> p (a b)")
        csb4 = bgflat[:, 0:512].rearrange("p (a b) -> p a b", b=128)
        Ecm4 = bgflat[:, 512:768].bitcast(BF16).rearrange("p (a b) -> p a b", b=128)
        ETc4 = bgflat[:, 768:1024].bitcast(BF16).rearrange("p (a b) -> p a b", b=128)
        impa = SB("impa", [128, 2, 32], F32)
        imp2 = SB("imp2", [128, 32], F32)
        top8 = SB("top8", [128, 8], F32)
        nmk = SB("nmk", [128, 32], BF16)
        nmT = SB("nmT", [32, 2, 128], BF16)
        kcT = SB("kcT", [128, 2, 128], BF16)
        vcaug = SB("vcaug", [128, 2, 161], BF16)
        w2b = SB("w2b", [128, 4, 128], BF16)
        posb = SB("posb", [128, 64], BF16)
        cconst = SB("cconst", [128, 2], F32)
        gT = SB("gT", [128, 2, 128], BF16)

        psm = [PSUM("psm%d" % i, [128, 512], F32) for i in range(2)]
        pss = [PSUM("pss%d" % i, [128, 512], F32) for i in range(2)]
        pso = [PSUM("pso%d" % i, [128, 512], F32) for i in range(2)]
        tpp = [PSUM("tpp%d" % i, [128, 1024], BF16) for i in range(2)]

        sbank = [pss[0], pss[1], psm[0], psm[1]]
        sbkey = [("pss", 0), ("pss", 1), ("psm", 0), ("psm", 1)]
        NSB, NPT, LOOK = 4, 5, 3
        tpn = [2]
        cnt = {"ssf": 0, "mm": 0, "ss": 0, "oo": 0, "tp": 0, "wb": 0, "ev": 0, "pt": 0, "ab": 0, "tmp": 0,
               "on": 0, "cb": 0}

        def nxt(name, n):
            v = cnt[name]
            cnt[name] = (v + 1) % n
            return v

        def dma(eng, out_ap, in_ap, reads, writes):
            return P.op(eng, lambda h: h.dma_start(out=out_ap, in_=in_ap), reads, writes, dma=True)

        def mm(out_ap, lhsT, rhs, start, stop, reads, writes):
            return P.op("pe", lambda h: h.matmul(out_ap, lhsT, rhs, start=start, stop=stop),
                        reads, writes)

        def tr(out_ap, in_ap, reads, writes):
            return P.op("pe", lambda h: h.transpose(out_ap, in_ap, identb[:]),
                        list(reads) + ["ident"], writes)

        def act(out_ap, in_ap, func, reads, writes, bias=None, scale=None, accum=None):
            kw = {}
            if accum is not None:
                kw["accum_out"] = accum
            if bias is not None:
                kw["bias"] = bias
            if scale is not None:
                kw["scale"] = scale
            return P.op("act", lambda h: h.activation(out=out_ap, in_=in_ap, func=func, **kw),
                        reads, writes)

        def v_tt(out, in0, in1, op, reads, writes, eng="dve"):
            return P.op(eng, lambda h: h.tensor_tensor(out=out, in0=in0, in1=in1, op=op),
                        reads, writes)

        def v_ts(out, in0, s1, s2, op0, op1, reads, writes, eng="dve"):
            if s2 is None:
                return P.op(eng, lambda h: h.tensor_scalar(out=out, in0=in0, scalar1=s1,
                                                           scalar2=None, op0=op0), reads, writes)
            return P.op(eng, lambda h: h.tensor_scalar(out=out, in0=in0, scalar1=s1, scalar2=s2,
                                                       op0=op0, op1=op1), reads, writes)

        def v_stt(out, in0, scalar, in1, op0, op1, reads, writes, eng="dve"):
            return P.op(eng, lambda h: h.scalar_tensor_tensor(out=out, in0=in0, scalar=scalar,
                                                              in1=in1, op0=op0, op1=op1),
                        reads, writes)

        def v_copy(out, in_, reads, writes, eng="dve"):
            return P.op(eng, lambda h: h.tensor_copy(out=out, in_=in_), reads, writes)

        def v_recip(out, in_, reads, writes):
            return P.op("dve", lambda h: h.reciprocal(out=out, in_=in_), reads, writes)

        def v_memset(ap, val, writes, eng="pool"):
            return P.op(eng, lambda h: h.memset(ap, val), [], writes)

        def evac(out, in_, reads, writes):
            if nxt("ev", 2) == 0:
                return P.op("act", lambda h: h.copy(out=out, in_=in_), reads, writes)
            return v_copy(out, in_, reads, writes)

        pref = {}

        def prefetch_slabs(sis):
            for si in sis:
                if si in pref:
                    continue
                s = nxt("wb", 6)
                dma("pool", WB[:, s, :], wr[:, si * 2048:(si + 1) * 2048], [], [("wb", s)])
                pref[si] = s

        def load_slabs(si, n):
            prefetch_slabs(range(si, si + n))
            return [pref.pop(si + j) for j in range(n)]

        def xkeys(name, t0, t1):
            return [(name, b) for b in range(t0 // 128, (t1 + 127) // 128)]

        def proj_fm(slot, xT, xname, ntok, post):
            for tc in range(ntok // 512):
                b = nxt("mm", 2)
                ps = psm[b]
                bk = ("psm", b)
                rk = [("wb", slot)] + xkeys(xname, tc * 512, tc * 512 + 512)
                for k in range(16):
                    mm(ps[:, 0:512], WB[:, slot, k * 128:(k + 1) * 128],
                       xT[:, k, tc * 512:(tc + 1) * 512], k == 0, k == 15, rk, [bk])
                post(ps, bk, tc)

        def post_copy(dst, dkey):
            def f(ps, bk, tc):
                evac(dst[:, tc * 512:(tc + 1) * 512], ps[:, 0:512], [bk], [dkey])
            return f

        def post_silu(dst, dkey):
            def f(ps, bk, tc):
                tb = nxt("tmp", 2)
                t = tmpf[:, tb, :]
                act(t, ps[:, 0:512], AF.Exp, [bk], [("tmpf", tb)], scale=-1.0)
                v_ts(t, t, 1.0, None, ALU.add, None, [("tmpf", tb)], [("tmpf", tb)])
                v_recip(t, t, [("tmpf", tb)], [("tmpf", tb)])
                v_tt(dst[:, tc * 512:(tc + 1) * 512], ps[:, 0:512], t, ALU.mult,
                     [bk, ("tmpf", tb)], [dkey])
            return f

        def proj_v(slot, vi, vkey, tmp, tmpkeys):
            for tc in range(4):
                b = nxt("mm", 2)
                ps = psm[b]
                bk = ("psm", b)
                rk = [("wb", slot)] + xkeys("xTf", tc * 512, tc * 512 + 512)
                for k in range(16):
                    mm(ps[:, 0:512], WB[:, slot, k * 128:(k + 1) * 128],
                       xTf[:, k, tc * 512:(tc + 1) * 512], k == 0, k == 15, rk, [bk])
                evac(tmp[:, tc * 512:(tc + 1) * 512], ps[:, 0:512], [bk], tmpkeys)
            for half in range(2):
                ti = nxt("tp", tpn[0])
                for bb in range(8):
                    blk = half * 8 + bb
                    tr(tpp[ti][:, bb * 128:(bb + 1) * 128], tmp[:, blk * 128:(blk + 1) * 128],
                       tmpkeys, [("tp", ti)])
                evac(vbuf[:, vi, half * 8:(half + 1) * 8, 0:128],
                     tpp[ti][:, 0:1024].rearrange("p (b d) -> p b d", d=128), [("tp", ti)], [vkey])

        dma("pool", identb[:], C("ident"), [], ["ident"])
        dma("sp", Uf[:], C("U"), [], ["Uf"])
        v_memset(onesf[:], 1.0, ["onesf"])
        dma("sp", bfb[:], C("bf"), [], ["bfb"])
        dma("sp", rb31[:], C("rb31"), [], ["rb31"])
        dma("sp", mkf[:], C("mk"), [], ["mkf"])
        o_mk = CO["mk"][0]
        dma("sp", cmb[:].rearrange("p a b -> p (a b)"), C("cmv"), [], ["cmb"])
        dma("sp", negm[:], C("cmm"), [], ["negm"])
        dma("sp", vmt[:].rearrange("p a b -> p (a b)"), C("vm"), [], ["vmt"])
        dma("sp", fbt[:].rearrange("p a b -> p (a b)"), C("fb"), [], ["fbt"])
        dma("pool", eexp[:], eexpd, [], ["eexp"])
        dma("pool", w2b[:].rearrange("p a b -> p (a b)"), w2d, [], ["w2b"])
        dma("pool", posb[:], posd, [], ["posb"])
        for g in range(2):
            dma("pool", vcaug[:, g, 129:161], C("ovl"), [], [("vcaug", g)])
        v_memset(vcaug[:, :, 128:129], 1.0, [("vcaug", 0), ("vcaug", 1)])
        v_memset(vbuf[:, :, :, 128:129], 1.0, [("vbuf", 0), ("vbuf", 1)])
        if stop_after == "p0":
            P.emit()
            return nc
        def load_transposed(src, nblk, dstT, kname):
            for blk in range(nblk):
                s = nxt("wb", 6)
                dma("pool", WB[:, s, :], src[blk * 128:(blk + 1) * 128, :], [], [("wb", s)])
                for half in range(2):
                    ti = nxt("tp", tpn[0])
                    tp = tpp[ti]
                    for kk in range(8):
                        k = half * 8 + kk
                        tr(tp[:, kk * 128:(kk + 1) * 128], WB[:, s, k * 128:(k + 1) * 128],
                           [("wb", s)], [("tp", ti)])
                    evac(dstT[:, half * 8:(half + 1) * 8, blk * 128:(blk + 1) * 128],
                         tp[:, 0:1024].rearrange("p (k t) -> p k t", t=128), [("tp", ti)],
                         [(kname, blk)])

        load_transposed(x_full, 16, xTf, "xTf")
        load_transposed(x_own, 8, xTo, "xTo")

        if stop_after == "p1":
            P.emit()
            return nc
        (ms,) = load_slabs(32, 1)
        b = nxt("mm", 2)
        ps = psm[b]
        for blk in range(16):
            for k in range(16):
                mm(ps[:, blk * 8:(blk + 1) * 8], xTf[:, k, blk * 128:(blk + 1) * 128],
                   WB[:, ms, k * 128:k * 128 + 8], k == 0, k == 15,
                   [("wb", ms), ("xTf", blk)], [("psm", b)])
        v_tt(ffb, ps[:, 0:128], bfb[:], ALU.add, [("psm", b), "bfb"], ["gx0"])
        act(ffb, ffb, AF.Exp, ["gx0"], ["gx0"], scale=-1.0)
        act(lfn, ffb, AF.Ln, ["gx0"], ["gx1"], bias=1.0)
        b1 = nxt("mm", 2)
        mm(psm[b1][:, 0:128], Uf[:], lfn, True, True, ["Uf", "gx1"], [("psm", b1)])
        b2 = nxt("mm", 2)
        mm(psm[b2][:, 0:128], onesf[:], lfn, True, True, ["onesf", "gx1"], [("psm", b2)])
        v_copy(tots, psm[b2][:, 0:128], [("psm", b2)], ["gx2"])
        v_memset(offn[:, 0:8], 0.0, ["offn"], eng="dve")
        for j in range(1, 16):
            v_tt(offn[:, j * 8:(j + 1) * 8], offn[:, (j - 1) * 8:j * 8], tots[:, (j - 1) * 8:j * 8],
                 ALU.add, ["offn", "gx2"], ["offn"])
        v_tt(cn, psm[b1][:, 0:128], offn[:], ALU.add, [("psm", b1), "offn"], ["gx3"])
        cn3 = cn.rearrange("p (j h) -> p j h", h=8)
        for h in range(8):
            for m in range(4):
                nj = 4 * m + 4
                v_ts(Bias[:, h, 2 * m * (m + 1):2 * m * (m + 1) + nj], cn3[:, 0:nj, h],
                     offn[:, 4 * m * 8 + h:4 * m * 8 + h + 1], None, ALU.subtract, None,
                     ["gx3", "offn"], ["Bias"])

        v_ts(negm[:], negm[:], -1.0, 30000.0, ALU.add, ALU.mult, ["negm"], ["negm"])
        for h in range(8):
            v_tt(cmb[:, h, :], cmb[:, h, :], negm[:], ALU.add, ["cmb", "negm"], ["cmb"])
        o_bg = CO["bg"][0]
        v_ts(bgs[:, 1, :], mkf[:], -1.0, -NEGM, ALU.add, ALU.mult, ["mkf"], [("bgs", 1)])
        v_copy(mkb[:].rearrange("p a b -> p (a b)"), bgs[:, 1, 384:640], [("bgs", 1)], ["mkb"])
        for h in range(8):
            dma("sp", bgs[:, 0, :], cst[:, o_bg + h * 640:o_bg + (h + 1) * 640], [], [("bgs", 0)])
            v_ts(bgs[:, 0, :], bgs[:, 0, :], rb31[:, h:h + 1], 1.0 / SCALE, ALU.subtract, ALU.mult,
                 [("bgs", 0), "rb31"], [("bgs", 0)])
            v_tt(bgs[:, 0, :], bgs[:, 0, :], mkf[:], ALU.mult, [("bgs", 0), "mkf"], [("bgs", 0)])
            v_tt(EB[:, h, :], bgs[:, 0, :], bgs[:, 1, :], ALU.add, [("bgs", 0), ("bgs", 1)], [("EB", h)])

        if stop_after == "p2":
            P.emit()
            return nc
        def mm_bank(items, bkey):
            for n_, (o_, l_, r_, rk_) in enumerate(items):
                mm(o_, l_, r_, n_ == 0, n_ == len(items) - 1, rk_, [bkey])

        def run_tasks(tasks, side=(), look=None):
            side = list(side)
            look = LOOK if look is None else look
            pend = None
            for k in range(min(look, len(tasks))):
                tasks[k]["S"]()
            for k, t in enumerate(tasks):
                if k + look < len(tasks):
                    tasks[k + look]["S"]()
                t["A"]()
                t["V"]()
                for _ in range(-(-len(side) // (len(tasks) - k))):
                    side.pop(0)()
                if pend is not None:
                    pend()
                    pend = None
                if t.get("end"):
                    t["end"]()
                pend = t.get("late")
            if pend is not None:
                pend()

        def finalize_head(mix, h, i, src_bf, src_key, szT, szkey):
            ti = nxt("tp", tpn[0])
            tp = tpp[ti]
            tr(tp[:, 0:128], src_bf, [src_key], [("tp", ti)])
            ob = nxt("ab", 2)
            v_tt(accb[:, ob, :], tp[:, 0:128], szT[:, i * 128:(i + 1) * 128], ALU.mult,
                 [("tp", ti), szkey], [("accb", ob)])
            dma("sp", ozs[mix, h, :, i * 128:(i + 1) * 128], accb[:, ob, :], [("accb", ob)],
                [("ozs", mix)])

        _FH = 8
        for h in range(_FH):
            sq, sk, sv, sz = load_slabs(4 * h, 4)
            proj_fm(sk, xTf, "xTf", 2048, post_copy(kbuf[:, 0, :], ("kbuf", 0)))
            proj_v(sv, 0, ("vbuf", 0), kbuf[:, 1, :], [("kbuf", 1)])
            proj_fm(sq, xTo, "xTo", 1024, post_copy(qbuf[:, 0, :], ("qbuf", 0)))
            proj_fm(sz, xTo, "xTo", 1024, post_silu(qbuf[:, 1, :], ("qbuf", 1)))
            if h + 1 < _FH:
                prefetch_slabs(range(4 * h + 4, 4 * h + 8))
            else:
                prefetch_slabs([32, 33, 39])
            tasks = []
            PTf = PT[:].rearrange("p a b c -> p a (b c)")
            for m in range(4):
                i0, i1 = 2 * m, 2 * m + 1
                ob0 = nxt("on", 2)
                ob1 = nxt("on", 2)
                groups = [[(j, 256) for j in (jg, jg + 1)] for jg in range(0, 4 * m + 2, 2)]
                groups.append([(4 * m + 2, 128), (4 * m + 3, 128)])
                for gi, grp in enumerate(groups):
                    si = nxt("ssf", 4)
                    pb = nxt("pt", NPT)
                    half = grp[0][1] == 128
                    W = grp[0][1]

                    def S_fn(grp=grp, si=si, i0=i0, i1=i1, W=W, m=m, half=half):
                        q0 = (i0 if W == 256 else i1) * 128
                        items = []
                        for bi, (j, w) in enumerate(grp):
                            items.append((sbank[si][:, bi * W:(bi + 1) * W], kbuf[:, 0, j * 128:(j + 1) * 128],
                                          qbuf[:, 0, q0:q0 + W], [("kbuf", 0), ("qbuf", 0)]))
                        if half or grp[0][0] == 4 * m:
                            for bi in range(2):
                                items.append((sbank[si][:, bi * W:bi * W + 128], identb[:], mkb[:, bi, :],
                                              ["ident", "mkb"]))
                        mm_bank(items, sbkey[si])

                    def A_fn(grp=grp, si=si, pb=pb, m=m, h=h, W=W, half=half):
                        for bi, (j, w) in enumerate(grp):
                            idx = 2 * m * (m + 1) + j
                            act(PTf[:, pb, bi * W:(bi + 1) * W], sbank[si][:, bi * W:(bi + 1) * W], AF.Exp,
                                [sbkey[si], "Bias"], [("PT", pb, bi)], bias=Bias[:, h, idx:idx + 1],
                                scale=SCALE)

                    def V_fn(grp=grp, pb=pb, m=m, W=W, i0=i0, i1=i1):
                        for bi, (j, w) in enumerate(grp):
                            qs = [(0, 0, 4 * m + 1), (1, 1, 4 * m + 3)] if W == 256 else [(0, 1, 4 * m + 3)]
                            for (qq, oi, jlast) in qs:
                                mm(pso[oi][:, 0:129], PTf[:, pb, bi * W + qq * 128:bi * W + qq * 128 + 128],
                                   vbuf[:, 0, j, :], j == 0, j == jlast,
                                   [("PT", pb, bi), ("vbuf", 0)], [("pso", oi)])

                    t = dict(S=S_fn, A=A_fn, V=V_fn)
                    fin = None
                    if (not half) and grp[0][0] == 4 * m:
                        fin = (0, i0, ob0)
                    elif half:
                        fin = (1, i1, ob1)
                    if fin is not None:
                        def E_fn(fin=fin):
                            oi, i_, ob = fin
                            rc = 7 * oi
                            v_recip(rinv[:, rc:rc + 1], pso[oi][:, 128:129], [("pso", oi)], [("rinvf", oi)])
                            v_ts(onorm[:, ob, :], pso[oi][:, 0:128], rinv[:, rc:rc + 1], None, ALU.mult, None,
                                 [("pso", oi), ("rinvf", oi)], [("onorm", ob)])

                        def L_fn(fin=fin, h=h):
                            oi, i_, ob = fin
                            finalize_head(0, h, i_, onorm[:, ob, :], ("onorm", ob), qbuf[:, 1, :], ("qbuf", 1))
                        t["end"] = E_fn
                        t["late"] = L_fn
                    tasks.append(t)
            run_tasks(tasks, look=3)

        if stop_after == "fox":
            P.emit()
            return nc

        (ms,) = load_slabs(32, 1)
        b = nxt("mm", 2)
        ps = psm[b]
        for i in range(8):
            for k in range(16):
                mm(ps[:, i * 24:(i + 1) * 24], xTo[:, k, i * 128:(i + 1) * 128],
                   WB[:, ms, k * 128 + 8:k * 128 + 32], k == 0, k == 15,
                   [("wb", ms), ("xTo", i)], [("psm", b)])
        act(G[:], ps[:, 0:192], AF.Exp, [("psm", b)], ["G"], scale=-1.0)
        v_ts(G[:], G[:], 1.0, None, ALU.add, None, ["G"], ["G"])
        v_recip(G[:], G[:], ["G"], ["G"])

        for kv in range(2):
            sl = [load_slabs(33 + 6 * g + kv, 1)[0] for g in range(2)]
            for g in range(2):
                proj_fm(sl[g], xTf, "xTf", 2048, post_copy(kbuf[:, g, :], ("kbuf", g)))
            w1s = []
            for q in range(4):
                s = nxt("wb", 6)
                dma("pool", WB[:, s, :], w1d[:, kv * 8192 + q * 2048:kv * 8192 + (q + 1) * 2048],
                    [], [("wb", s)])
                w1s.append(s)

            def w1ap(l, hc):
                s = w1s[l // 8]
                o = (l % 8) * 256 + hc * 128
                return WB[:, s, o:o + 128], ("wb", s)

            b = nxt("mm", 2)
            for hc in range(2):
                for l in range(32):
                    wap, wk = w1ap(l, hc)
                    mm(psm[b][:, hc:hc + 1], wap, posb[:, kv * 32 + l:kv * 32 + l + 1], l == 0, l == 31,
                       [wk, "posb"], [("psm", b)])
            v_copy(cconst[:], psm[b][:, 0:2], [("psm", b)], ["cconst"])
            for g in range(2):
                raw = kbuf[:, g, :].rearrange("p (c s) -> p c s", s=16)
                for hc in range(2):
                    b = nxt("mm", 2)
                    ps = psm[b]
                    for l in range(32):
                        wap, wk = w1ap(l, hc)
                        rhs = raw[:, 0:127, l] if l < 16 else raw[:, 1:128, l - 16]
                        mm(ps[:, 0:127], wap, rhs, l == 0, l == 31, [wk, ("kbuf", g)], [("psm", b)])
                    xs = gx[:, 0, 0:127]
                    x2 = gx[:, 1, 0:127]
                    v_ts(xs, ps[:, 0:127], cconst[:, hc:hc + 1], None, ALU.add, None,
                         [("psm", b), "cconst"], ["gx0"])
                    v_tt(x2, xs, xs, ALU.mult, ["gx0"], ["gx1"])
                    v_ts(x2, x2, 0.044715, 1.0, ALU.mult, ALU.add, ["gx1"], ["gx1"])
                    v_tt(x2, x2, xs, ALU.mult, ["gx1", "gx0"], ["gx1"])
                    act(x2, x2, AF.Exp, ["gx1"], ["gx1"], scale=-1.5957691216057308)
                    v_ts(x2, x2, 1.0, None, ALU.add, None, ["gx1"], ["gx1"])
                    v_recip(x2, x2, ["gx1"], ["gx1"])
                    v_tt(gT[:, hc, 0:127], xs, x2, ALU.mult, ["gx0", "gx1"], [("gT", hc)])
                b = nxt("mm", 2)
                ps = psm[b]
                if kv == 0:
                    for hc in range(2):
                        mm(ps[:, 0:127], w2b[:, hc, :], gT[:, hc, 0:127], hc == 0, hc == 1,
                           ["w2b", ("gT", hc)], [("psm", b)])
                    evac(kcT[:, g, 0:127], ps[:, 0:127], [("psm", b)], [("kcT", g)])
                else:
                    for hc in range(2):
                        mm(ps[0:127, 0:128], gT[:, hc, 0:127], w2b[:, 2 + hc, :], hc == 0, hc == 1,
                           ["w2b", ("gT", hc)], [("psm", b)])
                    evac(vcaug[0:127, g, 0:128], ps[0:127, 0:128], [("psm", b)], [("vcaug", g)])

        allx = [("xTf", b_) for b_ in range(16)]
        xflat = xTf[:].rearrange("p k t -> p (k t)")
        mT = xflat[:, 0:16384].rearrange("p (c t) -> p c t", t=1024)
        ozT = xflat[:, 16384:32768].rearrange("p (m k t) -> p m k t", m=2, k=8)
        kbf = kbuf[:].rearrange("p a b -> p (a b)")
        vbf = vbuf[:].rearrange("p a b c -> p (a b c)")
        wbf = WB[:].rearrange("p a b -> p (a b)")
        wslb = [kbf[:, 0:2048], kbf[:, 2048:4096], vbf[:, 0:2048]]
        wslk = [[("kbuf", 0)], [("kbuf", 1)], [("vbuf", 0), ("vbuf", 1)]]
        wsgb = [wbf[:, 0:4096], wbf[:, 4096:8192], wbf[:, 8192:12288]]
        wsgk = [[("wb", 0), ("wb", 1)], [("wb", 2), ("wb", 3)], [("wb", 4), ("wb", 5)]]
        allx = [("xTf", b_) for b_ in range(16)]
        xflat = xTf[:].rearrange("p k t -> p (k t)")
        mT = xflat[:, 0:16384].rearrange("p (c t) -> p c t", t=1024)
        ozT = xflat[:, 16384:32768].rearrange("p (m k t) -> p m k t", m=2, k=8)
        kbf = kbuf[:].rearrange("p a b -> p (a b)")
        vbf = vbuf[:].rearrange("p a b c -> p (a b c)")
        wbf = WB[:].rearrange("p a b -> p (a b)")
        wslb = [kbf[:, 0:2048], kbf[:, 2048:4096], vbf[:, 0:2048]]
        wslk = [[("kbuf", 0)], [("kbuf", 1)], [("vbuf", 0), ("vbuf", 1)]]
        wsgb = [wbf[:, 0:4096], wbf[:, 4096:8192], wbf[:, 8192:12288]]
        wsgk = [[("wb", 0), ("wb", 1)], [("wb", 2), ("wb", 3)], [("wb", 4), ("wb", 5)]]
        tpn[0] = 1
        cnt["tp"] = 0
        for g in range(2):
            s_ks, s_vs, s_kw, s_vw = load_slabs(33 + 6 * g + 2, 4)
            proj_fm(s_ks, xTf, "xTf", 2048, post_copy(kbuf[:, 0, :], ("kbuf", 0)))
            proj_v(s_vs, 0, ("vbuf", 0), kbuf[:, 1, :], [("kbuf", 1)])
            proj_v(s_vw, 1, ("vbuf", 1), qbuf[:, 0:2, :].rearrange("p a b -> p (a b)"),
                   [("qbuf", 0), ("qbuf", 1)])
            proj_fm(s_kw, xTf, "xTf", 2048, post_copy(kbuf[:, 1, :], ("kbuf", 1)))
            for hh in range(4):
                h = 4 * g + hh
                s_q, s_z = load_slabs(45 + 2 * h, 2)
                proj_fm(s_q, xTo, "xTo", 1024, post_copy(qbuf[:, hh, :], ("qbuf", hh)))
                proj_fm(s_z, xTo, "xTo", 1024, post_silu(qbuf[:, 4 + hh, :], ("qbuf", 4 + hh)))
            if g == 0:
                prefetch_slabs(range(33 + 6 + 2, 33 + 6 + 6))
            else:
                for cc in range(3):
                    dma("pool", wsgb[cc], wr[:, (61 + 2 * cc) * 2048:(63 + 2 * cc) * 2048], [],
                        [("wsg", cc)] + wsgk[cc])
                dma("sp", ozT[:, 0, :, :], ozs[0].rearrange("k p t -> p k t"), [("ozs", 0)],
                    [("ozT", 0)] + allx)
            accs = [acc0, gx]
            gxk = ["gx0", "gx1", "gx2", "gx3"]

            def cmp_sel_stages(i, g=g):
                ia = i % 2
                acc = accs[ia]
                CB = tpp[1][:].bitcast(F32)
                ck = ("tp", 1)
                w0 = 112 - 16 * i
                st = []

                def s1():
                    for hh in range(4):
                        mm(CB[:, hh * 128:hh * 128 + 127], qbuf[:, hh, i * 128:(i + 1) * 128], kcT[:, g, 0:127],
                           True, True, [("qbuf", hh), ("kcT", g)], [ck])
                st.append(s1)

                def s2():
                    for hh in range(4):
                        h = 4 * g + hh
                        v_stt(csb4[:, hh, 0:127], CB[:, hh * 128:hh * 128 + 127], SCALE, cmb[:, h, w0:w0 + 127],
                              ALU.mult, ALU.add, [ck, "cmb"], [("csb", hh), ("bgs", 0), ("bgs", 1)])
                        act(Ecm4[:, hh, 0:127], csb4[:, hh, 0:127], AF.Exp, [("csb", hh)], [("Ecm", hh)])
                st.append(s2)

                def s3():
                    ti = nxt("tp", tpn[0])
                    for hh in range(4):
                        tr(tpp[ti][0:127, hh * 128:(hh + 1) * 128], Ecm4[:, hh, 0:127], [("Ecm", hh)],
                           [("tp", ti)])
                    evac(ETc4[0:127, :, :], tpp[ti][0:127, 0:512].rearrange("p (a b) -> p a b", b=128),
                         [("tp", ti)], ["ETc"])
                st.append(s3)

                def oc(pair):
                    def f():
                        for hh in (2 * pair, 2 * pair + 1):
                            c0 = (hh % 2) * 256
                            mm(CB[:, c0:c0 + 161], ETc4[0:127, hh, :], vcaug[0:127, g, :], True, True,
                               ["ETc", ("vcaug", g)], [ck])
                    return f

                def dchain(pair):
                    def f():
                        for hh in (2 * pair, 2 * pair + 1):
                            h = 4 * g + hh
                            c0 = (hh % 2) * 256
                            v_ts(rinv[:, 1:2], CB[:, c0 + 128:c0 + 129], 1e-30, None, ALU.add, None, [ck],
                                 ["rinv1"])
                            v_recip(rinv[:, 1:2], rinv[:, 1:2], ["rinv1"], ["rinv1"])
                            v_tt(rinv[:, 2:3], rinv[:, 1:2], G[:, i * 24 + h * 3:i * 24 + h * 3 + 1], ALU.mult,
                                 ["rinv1", "G"], ["rinv2"])
                            v_ts(acc[:, hh, :], CB[:, c0:c0 + 128], rinv[:, 2:3], None, ALU.mult, None,
                                 [ck, "rinv2"], [("acc", ia, hh)] + (gxk if ia == 1 else []))
                            if hh == 0:
                                v_ts(impa[:, ia, :], CB[:, c0 + 129:c0 + 161], rinv[:, 1:2], None, ALU.mult,
                                     None, [ck, "rinv1"], [("impa", ia)])
                            else:
                                v_stt(impa[:, ia, :], CB[:, c0 + 129:c0 + 161], rinv[:, 1:2], impa[:, ia, :],
                                      ALU.mult, ALU.add, [ck, "rinv1", ("impa", ia)], [("impa", ia)])
                    return f
                st += [oc(0), dchain(0), oc(1), dchain(1)]

                def s6():
                    v_tt(imp2[:], impa[:, ia, :], vmt[:, i, :], ALU.mult, [("impa", ia), "vmt"], ["imp2"])
                    v_tt(imp2[:], imp2[:], fbt[:, i, :], ALU.add, ["imp2", "fbt"], ["imp2"])
                    P.op("dve", lambda hh_: hh_.max(out=top8[:], in_=imp2[:]), ["imp2"], ["top8"])
                    v_ts(nmk[:], imp2[:], top8[:, 7:8], NEGM, ALU.is_lt, ALU.mult, ["imp2", "top8"], ["nmk"])
                st.append(s6)

                def s7():
                    ti = nxt("tp", tpn[0])
                    tr(tpp[ti][0:32, 0:128], nmk[:], ["nmk"], [("tp", ti)])
                    evac(nmT[:, ia, :], tpp[ti][0:32, 0:128], [("tp", ti)], [("nmT", ia)])
                st.append(s7)
                return st

            for st_ in cmp_sel_stages(0):
                st_()
            for i in range(8):
                ia = i % 2
                acc = accs[ia]
                tasks = []
                for hh in range(4):
                    h = 4 * g + hh
                    nj = 2 * i + 2
                    oi = nxt("oo", 2)
                    for jg in range(0, nj, 4):
                        n = min(4, nj - jg)
                        si = nxt("ss", NSB)
                        pb = nxt("pt", NPT)

                        def S_fn(i=i, jg=jg, n=n, si=si, hh=hh, ia=ia):
                            qa = qbuf[:, hh, i * 128:(i + 1) * 128]
                            h = 4 * g + hh
                            items = []
                            for jj in range(n):
                                j = jg + jj
                                items.append((sbank[si][:, jj * 128:(jj + 1) * 128],
                                              kbuf[:, 0, j * 128:(j + 1) * 128], qa, [("kbuf", 0), ("qbuf", hh)]))
                            for jj in range(n):
                                j = jg + jj
                                items.append((sbank[si][:, jj * 128:(jj + 1) * 128],
                                              eexp[:, j * 128:(j + 1) * 128], nmT[:, ia, :],
                                              ["eexp", ("nmT", ia)]))
                            for jj in range(n):
                                r = jg + jj - (2 * i - 4)
                                if r >= 3:
                                    q = r - 1
                                    items.append((sbank[si][:, jj * 128:(jj + 1) * 128], identb[:],
                                                  EB[:, h, q * 128:(q + 1) * 128], ["ident", ("EB", h)]))
                            mm_bank(items, sbkey[si])

                        def A_fn(i=i, jg=jg, n=n, si=si, pb=pb, h=h):
                            act(PT[:, pb, 0:n, :], sbank[si][:, 0:n * 128].rearrange("p (a b) -> p a b", b=128),
                                AF.Exp, [sbkey[si]], [("PT", pb, jj) for jj in range(n)],
                                bias=rb31[:, h:h + 1], scale=SCALE)

                        def V_fn(jg=jg, n=n, pb=pb, oi=oi, nj=nj):
                            for jj in range(n):
                                j = jg + jj
                                mm(pso[oi][:, 0:129], PT[:, pb, jj, :], vbuf[:, 0, j, :], j == 0, j == nj - 1,
                                   [("PT", pb, jj), ("vbuf", 0)], [("pso", oi)])

                        t = dict(S=S_fn, A=A_fn, V=V_fn)
                        if jg + n == nj:
                            def E_fn(oi=oi, hh=hh, h=h, i=i, acc=acc, ia=ia):
                                v_recip(rinv[:, 3:4], pso[oi][:, 128:129], [("pso", oi)], ["rinv3"])
                                v_tt(rinv[:, 4:5], rinv[:, 3:4], G[:, i * 24 + h * 3 + 1:i * 24 + h * 3 + 2],
                                     ALU.mult, ["rinv3", "G"], ["rinv4"])
                                v_stt(acc[:, hh, :], pso[oi][:, 0:128], rinv[:, 4:5], acc[:, hh, :], ALU.mult,
                                      ALU.add, [("pso", oi), "rinv4", ("acc", ia, hh)], [("acc", ia, hh)])
                            t["end"] = E_fn
                        tasks.append(t)
                    j0 = max(0, 2 * i - 4)
                    js = list(range(j0, 2 * i + 2))
                    oi = nxt("oo", 2)
                    ob = nxt("on", 2)
                    for c0 in range(0, len(js), 4):
                        grp = js[c0:c0 + 4]
                        n = len(grp)
                        si = nxt("ss", NSB)
                        pb = nxt("pt", NPT)

                        def S_fn(i=i, grp=grp, si=si, hh=hh):
                            qa = qbuf[:, hh, i * 128:(i + 1) * 128]
                            h = 4 * g + hh
                            items = []
                            for jj, j in enumerate(grp):
                                items.append((sbank[si][:, jj * 128:(jj + 1) * 128],
                                              kbuf[:, 1, j * 128:(j + 1) * 128], qa, [("kbuf", 1), ("qbuf", hh)]))
                            for jj, j in enumerate(grp):
                                r = j - (2 * i - 4)
                                if r != 2:
                                    q = r if r < 2 else r - 1
                                    items.append((sbank[si][:, jj * 128:(jj + 1) * 128], identb[:],
                                                  EB[:, h, q * 128:(q + 1) * 128], ["ident", ("EB", h)]))
                            mm_bank(items, sbkey[si])

                        def A_fn(i=i, grp=grp, n=n, si=si, pb=pb, h=h):
                            act(PT[:, pb, 0:n, :], sbank[si][:, 0:n * 128].rearrange("p (a b) -> p a b", b=128),
                                AF.Exp, [sbkey[si]], [("PT", pb, jj) for jj in range(n)],
                                bias=rb31[:, h:h + 1], scale=SCALE)

                        def V_fn(grp=grp, pb=pb, oi=oi, js=js):
                            for jj, j in enumerate(grp):
                                mm(pso[oi][:, 0:129], PT[:, pb, jj, :], vbuf[:, 1, j, :], j == js[0], j == js[-1],
                                   [("PT", pb, jj), ("vbuf", 1)], [("pso", oi)])

                        t = dict(S=S_fn, A=A_fn, V=V_fn)
                        if grp[-1] == js[-1]:
                            def E_fn(oi=oi, ob=ob, hh=hh, h=h, i=i, acc=acc, ia=ia):
                                v_recip(rinv[:, 5:6], pso[oi][:, 128:129], [("pso", oi)], ["rinv5"])
                                v_tt(rinv[:, 6:7], rinv[:, 5:6], G[:, i * 24 + h * 3 + 2:i * 24 + h * 3 + 3],
                                     ALU.mult, ["rinv5", "G"], ["rinv6"])
                                v_stt(onorm[:, ob, :], pso[oi][:, 0:128], rinv[:, 6:7], acc[:, hh, :], ALU.mult,
                                      ALU.add, [("pso", oi), "rinv6", ("acc", ia, hh)], [("onorm", ob)])

                            def L_fn(i=i, ob=ob, h=h, hh=hh):
                                finalize_head(1, h, i, onorm[:, ob, :], ("onorm", ob), qbuf[:, 4 + hh, :],
                                              ("qbuf", 4 + hh))
                            t["end"] = E_fn
                            t["late"] = L_fn
                        tasks.append(t)
                run_tasks(tasks, side=(cmp_sel_stages(i + 1) if i + 1 < 8 else ()))

        tpn[0] = 2
        if stop_after == "nsa":
            P.emit()
            return nc

        dma("sp", ozT[:, 1, :, :], ozs[1].rearrange("k p t -> p k t"), [("ozs", 1)],
            [("ozT", 1)] + allx)
        ebt = EB[:].rearrange("p a b -> p (a b)").bitcast(F32)
        tm2 = [[tmpf[:, 0, :], tmpf[:, 1, :]], [ebt[:, 0:512], ebt[:, 512:1024]]]
        tk2 = [[("tmpf", 0), ("tmpf", 1)], [("ebt", 0), ("ebt", 1)]]
        ebk = [("EB", h_) for h_ in range(8)]
        tpf = [tpp[0][:].bitcast(F32), tpp[1][:].bitcast(F32)]
        bsets = [[(psm[0], ("psm", 0)), (psm[1], ("psm", 1)), (pss[0], ("pss", 0)), (pss[1], ("pss", 1))],
                 [(pso[0], ("pso", 0)), (pso[1], ("pso", 1)), (tpf[0], ("tp", 0)), (tpf[1], ("tp", 1))]]
        woA = xTo[:].rearrange("p k t -> p (k t)")
        woB1 = qbuf[:].rearrange("p a b -> p (a b)")
        woB2 = vbf[:, 2048:4096]
        woB3 = wbf[:, 0:6144]
        step = 0
        for cc in range(16):
            pb2 = cc % 3
            wsl, wsg = wslb[pb2], wsgb[pb2]
            if cc >= 3:
                dma("pool", wsg, wr[:, (61 + 2 * cc) * 2048:(63 + 2 * cc) * 2048], [], [("wsg", pb2)])
            dma("pool", wsl[:, 0:1024], wad[:, cc * 1024:(cc + 1) * 1024], [],
                [("wsl", pb2, 0)] + (wslk[pb2] if cc < 3 else []))
            dma("pool", wsl[:, 1024:2048], wbd[:, cc * 1024:(cc + 1) * 1024],
                [], [("wsl", pb2, 1)])
            if cc == 2:
                dma("pool", woB1, wod[:, 16384:24576], [], ["woB1"] + [("qbuf", q_) for q_ in range(8)])
                dma("pool", woB2, wod[:, 24576:26624], [], ["woB2"])
            for th in range(2):
                tsl = slice(th * 512, (th + 1) * 512)
                bs = bsets[step % 2]
                tms, tks = tm2[step % 2], tk2[step % 2]
                step += 1
                for mix in range(2):
                    bank, bk = bs[2 + mix]
                    for k in range(16):
                        o = mix * 2048 + k * 128
                        mm(bank[:, 0:512], wsg[:, o:o + 128], xTo[:, k, tsl], k == 0, k == 15,
                           [("wsg", pb2)] + xkeys("xTo", th * 512, th * 512 + 512), [bk])
                for mix in range(2):
                    bank, bk = bs[mix]
                    for k in range(8):
                        o = mix * 1024 + k * 128
                        mm(bank[:, 0:512], wsl[:, o:o + 128], ozT[:, mix, k, tsl], k == 0, k == 7,
                           [("wsl", pb2, mix), ("ozT", mix)], [bk])
                for mix in range(2):
                    act(tms[mix], bs[2 + mix][0][:, 0:512], AF.Sigmoid, [bs[2 + mix][1]], [tks[mix]] + ebk)
                v_tt(tms[0], tms[0], bs[0][0][:, 0:512], ALU.mult, [tks[0], bs[0][1]], [tks[0]])
                v_tt(tms[1], tms[1], bs[1][0][:, 0:512], ALU.mult, [tks[1], bs[1][1]], [tks[1]])
                v_tt(mT[:, cc, tsl], tms[0], tms[1], ALU.add, [tks[0], tks[1]],
                     [("mT", cc)] + allx)
        f32v = xflat[:, 16384:32768].bitcast(F32)
        lng = f32v[:, 0:2048]
        lnb = f32v[:, 2048:4096]
        xr = [f32v[:, 4096:6144], kbf.bitcast(F32)]
        yo = [f32v[:, 6144:8192], wbf[:, 6144:10240].bitcast(F32)]
        dma("sp", lng, C("lng"), [], ["lng", ("ozT", 0), ("ozT", 1)])
        dma("sp", lnb, C("lnb"), [], ["lnb"])
        xo_keys = [("xTo", b_) for b_ in range(8)]
        dma("pool", woB3, wod[:, 26624:32768], [], ["woB3"] + [("wsg", q_) for q_ in range(3)])
        dma("pool", woA, wod[:, 0:16384], [], ["woA"] + xo_keys)

        def wo_ap(k, c0):
            if k < 8:
                return woA[:, k * 2048 + c0:k * 2048 + c0 + 512], "woA"
            if k < 12:
                return woB1[:, (k - 8) * 2048 + c0:(k - 8) * 2048 + c0 + 512], "woB1"
            if k < 13:
                return woB2[:, c0:c0 + 512], "woB2"
            return woB3[:, (k - 13) * 2048 + c0:(k - 13) * 2048 + c0 + 512], "woB3"

        alpha = 2.0 ** 0.25
        for i in range(8):
            u = i % 2
            xres, yout = xr[u], yo[u]
            xk, yk = ("xres", u), ("yout", u)
            first = ([("wsg", q_) for q_ in range(3)] + [("wsl", q_, m_) for q_ in range(3) for m_ in range(2)]) \
                if i == 1 else []
            dma("sp", xres, x_own[i * 128:(i + 1) * 128, :], [], [xk] + first)
            bs = bsets[i % 2]
            korder = list(range(8, 13)) + list(range(13, 16)) + list(range(8))
            for cg in range(4):
                bank, bk = bs[cg]
                for kn, k in enumerate(korder):
                    wap, wk = wo_ap(k, cg * 512)
                    mm(bank[:, 0:512], mT[:, k, i * 128:(i + 1) * 128], wap, kn == 0, kn == 15,
                       [("mT", k), wk], [bk])
                v_stt(xres[:, cg * 512:(cg + 1) * 512], xres[:, cg * 512:(cg + 1) * 512], alpha,
                      bank[:, 0:512], ALU.mult, ALU.add, [xk, bk], [xk])
            c0_ = 4 * u
            s1, s2, nm_, t_ = (rinv[:, c0_:c0_ + 1], rinv[:, c0_ + 1:c0_ + 2], rinv[:, c0_ + 2:c0_ + 3],
                               rinv[:, c0_ + 3:c0_ + 4])
            lk = ("ln", u)
            act(yout, xres, AF.Identity, [xk], [yk, lk] + first, accum=s1)
            act(yout, xres, AF.Square, [xk], [yk, lk], accum=s2)
            v_ts(nm_, s1, -1.0 / 2048.0, None, ALU.mult, None, [lk], [lk])
            v_tt(t_, nm_, nm_, ALU.mult, [lk], [lk])
            v_stt(s2, s2, 1.0 / 2048.0, t_, ALU.mult, ALU.subtract, [lk], [lk])
            act(s2, s2, AF.Sqrt, [lk], [lk], bias=1e-5)
            v_recip(s2, s2, [lk], [lk])
            v_tt(nm_, nm_, s2, ALU.mult, [lk], [lk])
            act(yout, xres, AF.Identity, [xk, lk], [yk], bias=nm_, scale=s2)
            v_tt(yout, yout, lng, ALU.mult, [yk, "lng"], [yk])
            v_tt(yout, yout, lnb, ALU.add, [yk, "lnb"], [yk], eng="pool")
            dma("pool", outd[i * 128:(i + 1) * 128, :], yout, [yk], [("outd", i)])
        P.emit()
    return nc


def _rel_bucket(dist):
    n = np.maximum(dist, 0)
    exact = 16
    nf = np.maximum(n, 1).astype(np.float32)
    large = exact + (np.log(nf / np.float32(exact)) / np.float32(math.log(128 / 16))
                     * np.float32(16)).astype(np.int32)
    return np.where(n < exact, n, np.minimum(large, 31)).astype(np.int64)


def _shared_arrays(w_in, cmp_wk1, cmp_wk2, cmp_wv1, cmp_wv2, cmp_pos_k, cmp_pos_v, w_a, w_b, w_o):
    W = np.asarray(w_in[0], dtype=np.float32)
    cols = []
    for h in range(8):
        for nm in ("fq", "fk", "fv", "fz"):
            cols.append(np.arange(OFF[nm] + 128 * h, OFF[nm] + 128 * h + 128))
    cols.append(np.concatenate([np.arange(OFF["ff"], OFF["ff"] + 8), np.arange(OFF["ng"], OFF["ng"] + 24),
                                np.full(96, -1)]))
    for g in range(2):
        for nm in ("kc", "vc", "ks", "vs", "kw", "vw"):
            cols.append(np.arange(OFF[nm] + 128 * g, OFF[nm] + 128 * g + 128))
    for h in range(8):
        for nm in ("nq", "nz"):
            cols.append(np.arange(OFF[nm] + 128 * h, OFF[nm] + 128 * h + 128))
    for cc in range(16):
        for nm in ("ma", "mb"):
            cols.append(np.arange(OFF[nm] + 128 * cc, OFF[nm] + 128 * cc + 128))
    assert len(cols) == NSLAB
    cols = np.concatenate(cols)
    Wz = np.concatenate([W, np.zeros((2048, 1), np.float32)], axis=1)
    Wp = Wz[:, cols]
    wr = np.ascontiguousarray(Wp.reshape(16, 128, NSLAB, 128).transpose(1, 2, 0, 3)).reshape(128, -1)

    def w1l(w):
        return np.asarray(w[0], np.float32).reshape(32, 128, 256).transpose(1, 0, 2).reshape(128, 8192)

    w1 = np.ascontiguousarray(np.concatenate([w1l(cmp_wk1), w1l(cmp_wv1)], axis=1))

    def w2l(w):
        return np.asarray(w[0], np.float32).reshape(2, 128, 128).transpose(1, 0, 2).reshape(128, 256)

    w2 = np.ascontiguousarray(np.concatenate([w2l(cmp_wk2), w2l(cmp_wv2)], axis=1))
    posT = np.ascontiguousarray(np.concatenate([np.asarray(cmp_pos_k[0], np.float32).T,
                                                np.asarray(cmp_pos_v[0], np.float32).T], axis=1))

    def wab(w):
        return np.ascontiguousarray(np.asarray(w[0], np.float32).reshape(8, 128, 16, 128)
                                    .transpose(1, 2, 0, 3)).reshape(128, -1)

    wo = np.ascontiguousarray(np.asarray(w_o[0], np.float32).reshape(16, 128, 2048)
                              .transpose(1, 0, 2)).reshape(128, -1)
    eexp = (np.arange(2048)[None, :] // 64 == np.arange(32)[:, None]).astype(np.float32)
    return dict(wr=wr, w1=w1, w2=w2, posT=posT, wa=wab(w_a), wb=wab(w_b), wo=wo, eexp=eexp)


def _const_pack(z, b_f, rel_bias, ln_g, ln_b):
    rb = np.asarray(rel_bias, np.float32)
    cp = np.zeros((128, NF), np.float32)

    def put(name, arr):
        o, w = CO[name]
        cp[:, o:o + w] = np.asarray(arr, np.float32).reshape(128, w)

    put("bf", np.tile(np.asarray(b_f[0], np.float32), (128, 16)))
    put("rb31", np.tile(rb[31], (128, 1)))
    sl = np.arange(128)[:, None]
    tl = np.arange(128)[None, :]
    bg = np.zeros((128, 8, 5, 128), np.float32)
    mk = np.zeros((128, 5, 128), np.float32)
    for q, r in enumerate((0, 1, 3, 4, 5)):
        dist = 128 * (z + 4 - r) + tl - sl
        valid = (dist >= 0) & (dist < 512)
        bk = _rel_bucket(dist)
        mk[:, q, :] = valid
        for h in range(8):
            bg[:, h, q, :] = np.where(valid, rb[bk, h], 0.0)
    put("bg", bg)
    put("mk", mk)
    tlc = np.arange(128)[:, None]
    w = np.arange(239)[None, :]
    dist = 128 * z + tlc - 16 * (w - 112) - 31
    valid = dist >= 0
    bk = _rel_bucket(dist)
    cmv = np.zeros((128, 8, 239), np.float32)
    for h in range(8):
        cmv[:, h, :] = np.where(valid, rb[bk, h], 0.0)
    put("cmv", cmv)
    put("cmm", valid.astype(np.float32))
    vm = np.zeros((128, 8, 32), np.float32)
    fb = np.zeros((128, 8, 32), np.float32)
    j = np.arange(32)[None, :]
    for i in range(8):
        t = 128 * (2 * i + z) + np.arange(128)[:, None]
        cur = t // 64
        forced = (j == 0) | (j == cur) | (j == cur - 1)
        val = (64 * j) <= t
        vm[:, i, :] = val & ~forced
        fb[:, i, :] = np.where(val, np.where(forced, 1e30, 0.0), -1e30)
    put("vm", vm)
    put("fb", fb)
    c = np.arange(128)[:, None]
    ovl = ((16 * c < 64 * j + 64) & (16 * c + 32 > 64 * j) & (c < 127)).astype(np.float32)
    put("ovl", ovl)
    put("ident", np.eye(128, dtype=np.float32))
    put("U", (np.arange(128)[:, None] <= np.arange(128)[None, :]).astype(np.float32))
    put("lng", np.tile(np.asarray(ln_g[0], np.float32), (128, 1)))
    put("lnb", np.tile(np.asarray(ln_b[0], np.float32), (128, 1)))
    return cp


def _in_maps(x, shared, b_f, rel_bias, ln_g, ln_b):
    x = np.asarray(x, np.float32)
    packs = [_const_pack(z, b_f, rel_bias, ln_g, ln_b) for z in range(2)]
    maps = []
    for c in range(8):
        b, z = c // 2, c % 2
        xb = x[b]
        xo = np.ascontiguousarray(xb.reshape(8, 2, 128, 2048)[:, z].reshape(1024, 2048))
        m = dict(shared)
        m["x_full"] = np.ascontiguousarray(xb)
        m["x_own"] = xo
        m["cst"] = packs[z]
        maps.append(m)
    return maps


_NC_CACHE = {}


def kernel(x, w_in, b_f, cmp_pos_k, cmp_pos_v, cmp_wk1, cmp_wk2, cmp_wv1, cmp_wv2,
           w_a, w_b, w_o, ln_g, ln_b, rel_bias):
    shared = _shared_arrays(w_in, cmp_wk1, cmp_wk2, cmp_wv1, cmp_wv2, cmp_pos_k, cmp_pos_v,
                            w_a, w_b, w_o)
    maps = _in_maps(x, shared, b_f, rel_bias, ln_g, ln_b)
    nc = build_program()
    res = run_bass_kernel_spmd(nc, maps, core_ids=list(range(8)))
    out = np.zeros((4, 2048, 2048), np.float32)
    for c in range(8):
        b, z = c // 2, c % 2
        out[b].reshape(8, 2, 128, 2048)[:, z] = np.asarray(res.results[c]["out"]).reshape(8, 128, 2048)
    return out
```
